# Optimizing a Trainium2 kernel written in Bass

```python
import math
import jax, jax.numpy as jnp
from jax import lax
import numpy as np

D_MODEL = 1024
BATCH = 16
SEQ = 2048
DEPTH = 1

HEAD_DIM = 64
HEADS_PER_GROUP = 8
DILATED_GROUPS = ((128, 1), (512, 4), (2048, 16))
N_ATTN_HEADS = HEADS_PER_GROUP * len(DILATED_GROUPS)
ATTN_QKV_WIDTH = N_ATTN_HEADS * HEAD_DIM
ATTN_OUT_WIDTH = HEADS_PER_GROUP * HEAD_DIM
N_FOURIER_GROUPS = 4
FOURIER_GROUP_DIM = 128
FOURIER_WIDTH = N_FOURIER_GROUPS * FOURIER_GROUP_DIM
D_FF = 4 * D_MODEL
NUM_BUCKETS = 32
REL_MAX_DISTANCE = 1024
Q_OFF = 0
K_OFF = Q_OFF + ATTN_QKV_WIDTH
V_OFF = K_OFF + ATTN_QKV_WIDTH
F_OFF = V_OFF + ATTN_QKV_WIDTH
GA_OFF = F_OFF + FOURIER_WIDTH
GF_OFF = GA_OFF + D_MODEL
IN_WIDTH = GF_OFF + D_MODEL
DEEPNORM_ALPHA = (2 * DEPTH) ** 0.25
DEEPNORM_BETA = (8 * DEPTH) ** -0.25
LN_EPS = 1e-5
NEG_INF = -1e30

kernel_name = "hybrid_fourier_dilated_attn_encoder"


def _layer_norm(x, g, b):
    xf = x.astype(jnp.float32)
    mu = jnp.mean(xf, axis=-1, keepdims=True)
    var = jnp.mean(jnp.square(xf - mu), axis=-1, keepdims=True)
    return ((xf - mu) * lax.rsqrt(var + LN_EPS) * g.astype(jnp.float32) + b.astype(jnp.float32)).astype(x.dtype)


def _t5_buckets(rel):
    half = NUM_BUCKETS // 2
    ret = np.where(rel > 0, half, 0)
    n = np.abs(rel)
    max_exact = half // 2
    n_f = np.maximum(n, 1).astype(np.float32)
    large = max_exact + (np.log(n_f / max_exact) / math.log(REL_MAX_DISTANCE / max_exact) * (half - max_exact)).astype(np.int32)
    large = np.minimum(large, half - 1)
    return (ret + np.where(n < max_exact, n, large)).astype(np.int32)


def _dilated_window_attention(q, k, v, bias_table, window, dilation):
    B, S, H, Dh = q.shape
    d = dilation
    R = window // (2 * d)
    L = S // d
    nb = -(-L // R)
    Lp = nb * R
    pad = Lp - L

    def split(t):
        return t.reshape(B, L, d, H, Dh).transpose(0, 2, 3, 1, 4)

    qs = jnp.pad(split(q), ((0, 0), (0, 0), (0, 0), (0, pad), (0, 0)))
    qb = qs.reshape(B, d, H, nb, R, Dh)

    def band(t):
        t = jnp.pad(split(t), ((0, 0), (0, 0), (0, 0), (R, R + pad), (0, 0)))
        t = t.reshape(B, d, H, nb + 2, R, Dh)
        return jnp.concatenate([t[:, :, :, :-2], t[:, :, :, 1:-1], t[:, :, :, 2:]], axis=4)

    kb, vb = band(k), band(v)

    s_idx = np.arange(R)[:, None]
    t_idx = np.arange(3 * R)[None, :]
    rel = t_idx - R - s_idx
    k_pos = (np.arange(nb)[:, None] - 1) * R + np.arange(3 * R)[None, :]
    valid = (np.abs(rel)[None] <= R) & ((k_pos >= 0) & (k_pos < L))[:, None, :]
    buckets = _t5_buckets(rel * d)
    bias = jnp.transpose(bias_table[buckets].astype(jnp.float32), (2, 0, 1))

    logits = jnp.einsum('bghnqd,bghnkd->bghnqk', qb, kb).astype(jnp.float32) * (Dh ** -0.5)
    logits = logits + bias[None, None, :, None]
    logits = jnp.where(jnp.asarray(valid)[None, None, None], logits, NEG_INF)
    m = jnp.max(logits, axis=-1, keepdims=True)
    p = jnp.exp(logits - m)
    denom = jnp.sum(p, axis=-1, keepdims=True)
    out = jnp.einsum('bghnqk,bghnkd->bghnqd', p, vb.astype(jnp.float32)) / denom
    lse = (m + jnp.log(denom))[..., 0]

    out = out.reshape(B, d, H, Lp, Dh)[:, :, :, :L].transpose(0, 3, 1, 2, 4).reshape(B, S, H, Dh)
    lse = lse.reshape(B, d, H, Lp)[:, :, :, :L].transpose(0, 3, 1, 2).reshape(B, S, H)
    return out, lse


def _fourier_mix(u):
    B, S, _ = u.shape
    ug = u.astype(jnp.float32).reshape(B, S, N_FOURIER_GROUPS, FOURIER_GROUP_DIM)
    f = jnp.real(jnp.fft.fft2(ug, axes=(1, 3), norm="ortho"))
    return f.reshape(B, S, FOURIER_WIDTH).astype(u.dtype)


def _token_mixing(x, rel_bias, w_in, w_fourier_out, w_attn_out, w_out):
    B, S, _ = x.shape
    proj = jnp.einsum('bsd,de->bse', x, w_in)
    q = proj[..., Q_OFF:K_OFF].reshape(B, S, N_ATTN_HEADS, HEAD_DIM)
    k = proj[..., K_OFF:V_OFF].reshape(B, S, N_ATTN_HEADS, HEAD_DIM)
    v = proj[..., V_OFF:F_OFF].reshape(B, S, N_ATTN_HEADS, HEAD_DIM)
    u_f = proj[..., F_OFF:GA_OFF]
    gate_a = jax.nn.sigmoid(proj[..., GA_OFF:GF_OFF])
    gate_f = jax.nn.sigmoid(proj[..., GF_OFF:IN_WIDTH])

    outs, lses = [], []
    for g, (window, dilation) in enumerate(DILATED_GROUPS):
        hs = slice(g * HEADS_PER_GROUP, (g + 1) * HEADS_PER_GROUP)
        o, l = _dilated_window_attention(q[:, :, hs], k[:, :, hs], v[:, :, hs], rel_bias[:, hs], window, dilation)
        outs.append(o)
        lses.append(l)
    weights = jax.nn.softmax(jnp.stack(lses, axis=0), axis=0)[..., None]
    attn = jnp.sum(weights * jnp.stack(outs, axis=0), axis=0).reshape(B, S, ATTN_OUT_WIDTH).astype(x.dtype)
    y_attn = jnp.einsum('bse,ed->bsd', attn, w_attn_out)

    y_four = jnp.einsum('bse,ed->bsd', _fourier_mix(u_f), w_fourier_out)

    merged = gate_a * y_attn + gate_f * y_four
    return jnp.einsum('bsd,de->bse', merged, w_out)


def _sq_relu_mlp(x, w_up, b_up, w_down, b_down):
    h = jnp.square(jax.nn.relu(jnp.einsum('bsd,df->bsf', x, w_up) + b_up))
    return jnp.einsum('bsf,fd->bsd', h, w_down) + b_down


def setup_inputs(seed: int = 0) -> dict:
    key = jax.random.key(seed)
    ks = jax.random.split(key, 16)
    f32 = jnp.float32
    nrm = lambda k, shape, scale: jax.random.normal(k, shape, f32) * scale
    return {
        "x": nrm(ks[0], (BATCH, SEQ, D_MODEL), 1.0),
        "rel_bias": nrm(ks[1], (NUM_BUCKETS, N_ATTN_HEADS), 0.1),
        "w_in": nrm(ks[2], (DEPTH, D_MODEL, IN_WIDTH), D_MODEL ** -0.5),
        "w_fourier_out": nrm(ks[3], (DEPTH, FOURIER_WIDTH, D_MODEL), DEEPNORM_BETA * FOURIER_WIDTH ** -0.5),
        "w_attn_out": nrm(ks[4], (DEPTH, ATTN_OUT_WIDTH, D_MODEL), DEEPNORM_BETA * ATTN_OUT_WIDTH ** -0.5),
        "w_out": nrm(ks[5], (DEPTH, D_MODEL, D_MODEL), DEEPNORM_BETA * D_MODEL ** -0.5),
        "ln1_g": 1.0 + nrm(ks[6], (DEPTH, D_MODEL), 0.02),
        "ln1_b": nrm(ks[7], (DEPTH, D_MODEL), 0.02),
        "w_up": nrm(ks[8], (DEPTH, D_MODEL, D_FF), D_MODEL ** -0.5),
        "b_up": nrm(ks[9], (DEPTH, D_FF), 0.02),
        "w_down": nrm(ks[10], (DEPTH, D_FF, D_MODEL), DEEPNORM_BETA * D_FF ** -0.5),
        "b_down": nrm(ks[11], (DEPTH, D_MODEL), 0.02),
        "ln2_g": 1.0 + nrm(ks[12], (DEPTH, D_MODEL), 0.02),
        "ln2_b": nrm(ks[13], (DEPTH, D_MODEL), 0.02),
    }


def reference(x, rel_bias, w_in, w_fourier_out, w_attn_out, w_out, ln1_g, ln1_b, w_up, b_up, w_down, b_down, ln2_g, ln2_b):
    h = x
    for layer in range(DEPTH):
        mix = _token_mixing(h, rel_bias, w_in[layer], w_fourier_out[layer], w_attn_out[layer], w_out[layer])
        h = _layer_norm(DEEPNORM_ALPHA * h + mix, ln1_g[layer], ln1_b[layer])
        ff = _sq_relu_mlp(h, w_up[layer], b_up[layer], w_down[layer], b_down[layer])
        h = _layer_norm(DEEPNORM_ALPHA * h + ff, ln2_g[layer], ln2_b[layer])
    return h
```

```python
import numpy as np
import ml_dtypes
import concourse.bass as bass
import concourse.mybir as mybir
from concourse.bass_utils import run_bass_kernel_spmd

F32 = mybir.dt.float32
BF16 = mybir.dt.bfloat16
AF = mybir.ActivationFunctionType
ALU = mybir.AluOpType

NCORES = 8
S = 2048
D = 1024
NSEQ = 2
DFF = 4096
ALPHA = 2.0 ** 0.25
EPS = 1e-5
NEG = -30000.0
GROUPS = ((128, 1), (512, 4), (2048, 16))
POOL_BYTES = 206 * 1024


class Res:
    __slots__ = ("name", "writers", "readers", "inherit", "excl")

    def __init__(self, name, inherit=(), excl=False):
        self.name = name
        self.excl = excl
        self.writers = []
        self.readers = []
        self.inherit = list(inherit)


class Op:
    __slots__ = ("eng", "fn", "deps", "dma", "sem", "val", "signal", "idx")


class KB:
    ENGS = ("pe", "act", "dve", "pool", "sp")

    def __init__(self, nc):
        self.nc = nc
        self.ops = {e: [] for e in self.ENGS}
        self.n = 0
        self.ndma = 0
        self.NDMASEM = 12
        self.nsw = 0
        self.dma_last = [None] * self.NDMASEM
        self.dma_cnt = [0] * self.NDMASEM
        self.out_dmas = []

    def op(self, eng, fn, reads=(), writes=(), dma=False, is_out=False):
        o = Op()
        o.eng, o.fn, o.dma, o.signal, o.idx = eng, fn, dma, False, self.n
        self.n += 1
        deps = []
        for r in reads:
            deps += r.inherit
            deps += r.writers
            if r.excl:
                deps += [q for q in r.readers if q.eng != eng]
        for r in writes:
            deps += r.inherit
            deps += r.writers
            deps += r.readers
        if dma and eng == "pool":
            o.sem, o.val = ("sw", self.nsw), 16
            self.nsw += 1
            o.signal = True
        elif dma:
            j = self.ndma % self.NDMASEM
            self.ndma += 1
            if self.dma_last[j] is not None:
                deps.append(self.dma_last[j])
            self.dma_cnt[j] += 1
            o.sem, o.val = j, 16 * self.dma_cnt[j]
            self.dma_last[j] = o
            o.signal = True
        else:
            o.sem, o.val = None, None
        seen = set()
        o.deps = []
        for d in deps:
            if d.idx in seen or d is o:
                continue
            seen.add(d.idx)
            if d.eng == "pe" and eng == "pe" and not d.dma:
                continue
            o.deps.append(d)
            d.signal = True
        for r in reads:
            r.readers.append(o)
        for r in writes:
            r.writers = [o]
            r.readers = []
            r.inherit = []
        self.ops[eng].append(o)
        if is_out:
            self.out_dmas.append(o)
        return o

    def emit(self):
        nc = self.nc
        esem = {e: nc.alloc_semaphore("s_" + e) for e in self.ENGS}
        dsem = {j: nc.alloc_semaphore("d_%d" % j) for j in range(self.NDMASEM)}
        for j in range(self.nsw):
            dsem[("sw", j)] = nc.alloc_semaphore("w_%d" % j)
        for e in self.ENGS:
            t = 0
            for o in self.ops[e]:
                if o.dma:
                    continue
                if o.signal:
                    t += 1
                    o.val = t
        final_waits = [(dsem[o.sem], o.val) for o in self.out_dmas]

        def run(e, eng):
            waited = {}
            for o in self.ops[e]:
                for d in o.deps:
                    if d.dma:
                        key, sem, val = ("d", d.sem), dsem[d.sem], d.val
                    else:
                        key, sem, val = ("e", d.eng), esem[d.eng], d.val
                    if waited.get(key, 0) >= val:
                        continue
                    waited[key] = val
                    eng.wait_ge(sem, val)
                ins = o.fn(eng)
                if o.dma:
                    ins.then_inc(dsem[o.sem], 16)
                elif o.signal:
                    ins.then_inc(esem[e], 1)
            if e == "sp":
                for sem, val in final_waits:
                    eng.wait_ge(sem, val)

        with nc.Block() as block:
            @block.sync
            def _(eng):
                run("sp", eng)

            @block.scalar
            def _(eng):
                run("act", eng)

            @block.vector
            def _(eng):
                run("dve", eng)

            @block.gpsimd
            def _(eng):
                run("pool", eng)

            @block.tensor
            def _(eng):
                run("pe", eng)


class Mem:
    def __init__(self, nc):
        self.pool = nc.alloc_sbuf_tensor("pool", [128, POOL_BYTES // 4], F32)
        self.regions = []
        self.top = 0
        self.htop = POOL_BYTES
        self.marks = []

    def mark(self):
        self.marks.append((self.top, len(self.regions)))

    def release(self):
        top, n = self.marks.pop()
        for i in range(n, len(self.regions)):
            self.regions[i][3] = False
        self.top = top

    def alloc(self, name, nbytes, nres=1, high=False):
        nbytes = (nbytes + 31) // 32 * 32
        if high:
            self.htop -= nbytes
            start = self.htop
        else:
            start = self.top
            self.top += nbytes
        assert self.top <= self.htop, (name, self.top, self.htop, nbytes)
        inherit = []
        for (s, e, rs, alive) in self.regions:
            if (not alive) and s < start + nbytes and start < e:
                for r in rs:
                    inherit += r.writers + r.readers + r.inherit
        rs = [Res("%s_%d" % (name, i), inherit) for i in range(nres)]
        self.regions.append([start, start + nbytes, rs, True])
        return start, rs

    def add_late(self, ops, start, end):
        for (s, e, rs, alive) in self.regions:
            if alive and s < end and start < e:
                for r in rs:
                    r.inherit += ops

    def view(self, off, dtype, shape, parts=128, p0=0):
        n = 1
        for s_ in shape:
            n *= s_
        assert off % 4 == 0
        if dtype == F32:
            ap = self.pool[p0:p0 + parts, off // 4: off // 4 + n]
        else:
            assert n % 2 == 0
            ap = self.pool[p0:p0 + parts, off // 4: off // 4 + n // 2].bitcast(BF16)
        if len(shape) == 1:
            return ap
        names = ["a%d" % i for i in range(len(shape))]
        pat = "p (%s) -> p %s" % (" ".join(names), " ".join(names))
        kw = {names[i]: shape[i] for i in range(1, len(shape))}
        return ap.rearrange(pat, **kw)


def t5_buckets(rel):
    half = 16
    ret = np.where(rel > 0, half, 0)
    n = np.abs(rel)
    max_exact = half // 2
    n_f = np.maximum(n, 1).astype(np.float32)
    large = max_exact + (np.log(n_f / max_exact) / np.log(1024 / max_exact) * (half - max_exact)).astype(np.int32)
    large = np.minimum(large, half - 1)
    return (ret + np.where(n < max_exact, n, large)).astype(np.int32)


def build(debug=None, nseq=NSEQ):
    nc = bass.Bass("TRN2", target_bir_lowering=False)
    kb = KB(nc)
    mem = Mem(nc)

    def din(name, shape, dt=F32):
        return nc.dram_tensor(name, list(shape), dt, kind="ExternalInput").ap()

    x_d = din("x", [NSEQ, S, D])
    wqk_d = din("wqk", [12, 2, 128, 8, 128])
    wv_d = din("wv", [3, 128, 8, 512])
    wf_d = din("wf", [128, 8, 512])
    wg_d = din("wg", [16, 128, 8, 128])
    wao_d = din("wao", [128, 4, D])
    wfo_d = din("wfo", [128, 4, D])
    wout_d = din("wout", [128, 8, D])
    wup_d = din("wup", [8, 128, 8, 512])
    wdn_d = din("wdn", [8, 128, 32, 128])
    tab_d = din("tab", [128, 24, 256])
    dft_d = din("dft", [4, 128, 2, 8, 512])
    c1024_d = din("c1024", [1, S])
    ccsc_d = din("ccsc", [128, 256])
    identf_d = din("identf", [128, 128])
    identb_d = din("identb", [128, 128], BF16)
    pvec_d = din("pvec", [128, 72])
    g2b_d = din("g2b", [128, D])
    b2b_d = din("b2b", [128, D])
    out_d = nc.dram_tensor("out", [NSEQ, S, D], F32, kind="ExternalOutput").ap()
    dbg_d = None
    if debug is not None:
        dbg_d = nc.dram_tensor("dbg", list(debug[1]), debug[2], kind="ExternalOutput").ap()

    banks = [nc.alloc_psum_tensor("bank%d" % i, [128, 512], F32) for i in range(8)]
    bres = [Res("bank%d" % i, excl=True) for i in range(8)]
    bank_free = list(range(8))

    def next_bank():
        i = bank_free.pop(0)
        return banks[i], bres[i]

    def bfree(br):
        i = bres.index(br)
        assert i not in bank_free
        bank_free.append(i)

    def palloc(name, dtype, shape, parts=128, nres=1, high=False):
        n = 1
        for s_ in shape:
            n *= s_
        off, rs = mem.alloc(name, n * (4 if dtype == F32 else 2), nres, high)
        return mem.view(off, dtype, shape, parts), rs, off

    identf, identf_r, _ = palloc("identf", F32, [128])
    identb, identb_r, _ = palloc("identb", BF16, [128])
    onesf, onesf_r, _ = palloc("onesf", F32, [64])
    ccsc, ccsc_r, _ = palloc("ccsc", BF16, [256])
    pvec, pvec_r, _ = palloc("pvec", F32, [72])
    g2b, g2b_r, _ = palloc("g2b", F32, [D])
    b2b, b2b_r, _ = palloc("b2b", F32, [D])
    mhalf, mhalf_r, _ = palloc("mhalf", F32, [8])

    def dma(eng, out, in_, reads=(), writes=(), is_out=False):
        return kb.op(eng, lambda e: e.dma_start(out=out, in_=in_), reads, writes, dma=True, is_out=is_out)

    dma("sp", identf, identf_d, writes=identf_r)
    dma("sp", identb, identb_d, writes=identb_r)
    dma("pool", ccsc, ccsc_d, writes=ccsc_r)
    dma("sp", pvec, pvec_d, writes=pvec_r)
    dma("sp", g2b, g2b_d, writes=g2b_r)
    dma("sp", b2b, b2b_d, writes=b2b_r)
    kb.op("dve", lambda e: e.memset(onesf, 1.0), writes=onesf_r)
    kb.op("dve", lambda e: e.memset(mhalf, -0.5), writes=mhalf_r)
    G1, B1, AG1, AB1, BDN, BUP = 0, 8, 16, 24, 32, 40
    kb.op("dve", lambda e: e.tensor_scalar(out=pvec[:, 16:32], in0=pvec[:, 0:16], scalar1=ALPHA, scalar2=None,
                                           op0=ALU.mult), pvec_r, pvec_r)

    evac_rr = [0]

    def evac_copy(out, in_, reads, writes, scale=None, eng=None):
        if eng is None:
            eng = ("act", "dve")[evac_rr[0] % 2]
            evac_rr[0] += 1
        if eng == "act":
            if scale is None:
                return kb.op("act", lambda e: e.activation(out=out, in_=in_, func=AF.Copy), reads, writes)
            return kb.op("act", lambda e: e.activation(out=out, in_=in_, func=AF.Copy, scale=scale), reads, writes)
        if scale is None:
            return kb.op("dve", lambda e: e.tensor_copy(out=out, in_=in_), reads, writes)
        return kb.op("dve", lambda e: e.tensor_scalar(out=out, in0=in_, scalar1=scale, scalar2=None, op0=ALU.mult),
                     reads, writes)

    def mm(out, lhsT, rhs, start, stop, reads, writes, skip=False):
        if skip:
            return kb.op("pe", lambda e: e.matmul(out, lhsT, rhs, start=start, stop=stop, skip_group_check=True),
                         reads, writes)
        return kb.op("pe", lambda e: e.matmul(out, lhsT, rhs, start=start, stop=stop), reads, writes)

    def tr(out, in_, reads, writes):
        return kb.op("pe", lambda e: e.transpose(out, in_, identf), list(reads) + identf_r, writes)


    class Ring:
        def __init__(self, n, nbuf, load):
            self.n, self.nbuf, self.load, self.nxt = n, nbuf, load, 0

        def need(self, k):
            lim = min(self.n - 1, k + self.nbuf - 1)
            while self.nxt <= lim:
                self.load(self.nxt, self.nxt % self.nbuf)
                self.nxt += 1

    wout_s = nc.dram_tensor("wout_s", [128, 8 * D], BF16).ap()
    wup_s = nc.dram_tensor("wup_s", [8, 128, 8 * 512], BF16).ap()
    wdn_s = nc.dram_tensor("wdn_s", [8, 128, 32 * 128], BF16).ap()
    wout_sr = [Res("wout_s")]
    wup_sr = [Res("wup_s%d" % i) for i in range(8)]
    wdn_sr = [Res("wdn_s%d" % i) for i in range(8)]

    cast_jobs = [(wout_s, wout_d.rearrange("p c n -> p (c n)"), wout_sr)]
    for i in range(8):
        cast_jobs.append((wup_s[i], wup_d[i].rearrange("p c n -> p (c n)"), [wup_sr[i]]))
    for i in range(8):
        cast_jobs.append((wdn_s[i], wdn_d[i].rearrange("p c n -> p (c n)"), [wdn_sr[i]]))
    dft_s = din("dft_s", [4, 128, 2 * 8 * 512], BF16)
    dft_sr = [Res("dft_s%d" % i) for i in range(4)]
    for i in range(4):
        cast_jobs.insert(1 + 2 * i, (dft_s[i], dft_d[i].rearrange("p a c n -> p (a c n)"), [dft_sr[i]]))

    def emit_scratch_casts(n):
        for _ in range(n):
            if cast_jobs:
                o_, i_, r_ = cast_jobs.pop(0)
                dma("pool", o_, i_, writes=r_)

    den_s = nc.dram_tensor("den_s", [1, 2 * S], F32).ap()
    rden_s = nc.dram_tensor("rden_s", [1, 2 * S], F32).ap()
    den_sr = [Res("den_s")]
    rden_sr = [Res("rden_s")]

    class _Stop(Exception):
        pass

    def finish_debug(ap, rs):
        dma("sp", dbg_d, ap, reads=rs, is_out=True)
        raise _Stop()

    late_stores = []
    try:
        for s in range(nseq):
            mem.mark()
            mem.mark()
            xT, xT_r, _ = palloc("xT", BF16, [8, S], nres=16)
            attnT, attnT_r, _ = palloc("attnT", BF16, [4, S], nres=4)
            wf, wf_r, _ = palloc("wf", BF16, [8, 512], nres=1)

            xs, xs_r, _ = palloc("xs", F32, [3, D], nres=3, high=True)
            xs_ring = Ring(16, 3, lambda k, b: dma("sp", xs[:, b, :], x_d[s, k * 128:(k + 1) * 128, :],
                                                   writes=[xs_r[b]]))
            xs_ring.need(0)
            if late_stores:
                n0 = kb.n
                for fn_ in late_stores[0]:
                    fn_()
                late_ops = [o for o in kb.ops["sp"] if o.idx >= n0]
                mem.add_late(late_ops, late_stores[1], late_stores[2])
                late_stores.clear()
            mem.mark()
            tab, tab_r, _ = palloc("tab", BF16, [24, 512])
            V, V_r, _ = palloc("V", BF16, [3, 16, 8, 65], nres=3)
            Vm = V.rearrange("p g c h e -> p (g c h) e")[:, :, 64:65]
            kb.op("dve", (lambda Vm: lambda e: e.memset(Vm, 1.0))(Vm), writes=V_r)
            mem.mark()
            wv, wv_r, _ = palloc("wv", BF16, [2, 8, 512], nres=2)
            wv_ring = Ring(3, 2, lambda k, b: dma("pool", wv[:, b], wv_d[k], writes=[wv_r[b]]))
            wv_ring.need(0)
            tabraw, tabraw_r, _ = palloc("tabraw", BF16, [24, 256])

            def v_chunk(g, d, nkc, c):
                seg, kc = c // nkc, c % nkc
                t0 = (128 * kc) * d + seg
                bk, br = next_bank()
                for dc in range(8):
                    if d == 1:
                        lhsT = xT[:, dc, t0:t0 + 128]
                    else:
                        lhsT = xT[:, dc, t0:t0 + 127 * d + 1:d]
                    mm(bk[:, :], lhsT, wv[:, g % 2, dc, :], dc == 0, dc == 7, xT_r + [wv_r[g % 2]], [br])
                evac_copy(V[:, g, c, :, 0:64], bk[:, :].rearrange("p (h e) -> p h e", h=8), [br], [V_r[g]])
                bfree(br)

            for tt in range(16):
                xs_ring.need(tt)
                b = tt % 3
                for half in range(2):
                    bk, br = next_bank()
                    for j in range(4):
                        dc = half * 4 + j
                        tr(bk[:, j * 128:(j + 1) * 128], xs[:, b, dc * 128:(dc + 1) * 128], [xs_r[b]], [br])
                    evac_copy(xT[:, half * 4:half * 4 + 4, tt * 128:(tt + 1) * 128],
                              bk[:, :].rearrange("p (j t) -> p j t", j=4), [br], [xT_r[tt]])
                    bfree(br)
                if tt >= 1:
                    v_chunk(0, 1, 16, tt - 1)
            v_chunk(0, 1, 16, 15)
            if debug is not None and debug[0] == "xT" and s == 0:
                finish_debug(xT, xT_r)
            dma("pool", tabraw, tab_d, writes=tabraw_r)
            for i in range(2):
                kb.op("act", (lambda i: lambda e: e.activation(out=tab[:, 0:16, 256 * i:256 * i + 256],
                                                                in_=tabraw[:, 0:16, :], func=AF.Exp))(i),
                      tabraw_r, tab_r)
            for i in range(4):
                kb.op("act", (lambda i: lambda e: e.activation(out=tab[:, 16:24, 128 * i:128 * i + 128],
                                                                in_=tabraw[:, 16:24, 64:192], func=AF.Exp))(i),
                      tabraw_r, tab_r)
            for g, (win, d) in enumerate(GROUPS):
                if g == 0:
                    continue
                L = S // d
                nkc = L // 128
                wv_ring.need(g)
                for c in range(16):
                    v_chunk(g, d, nkc, c)
            for reg in mem.regions:
                if reg[0] >= mem.htop:
                    reg[3] = False
            mem.htop = POOL_BYTES
            mem.release()
            dma("pool", wf, wf_d, writes=wf_r)
            acc, acc_r, _ = palloc("acc", F32, [2, S], parts=65, nres=2)
            qk, qk_r, _ = palloc("qk", BF16, [2, 2, S], nres=4)
            PT, PT_r, _ = palloc("PT", BF16, [4, 512], nres=4)
            wqk, wqk_r, _ = palloc("wqk", BF16, [2, 2, 8, 128], nres=2)
            pair_order = [4 * g + pp for pp in range(4) for g in range(3)]
            def load_wqk(k, b):
                dma("pool", wqk[:, b], wqk_d[pair_order[k]].rearrange("a p c n -> p a c n"), writes=[wqk_r[b]])
                if k >= 1:
                    emit_scratch_casts(2)

            wqk_ring = Ring(12, 2, load_wqk)
            rd, rd_r, _ = palloc("rd", F32, [32])
            rbc, rbc_r, _ = palloc("rbc", F32, [2, S], parts=64)

            def norm_dma():
                dma("sp", den_s, acc[64:65, :, :].rearrange("o e t -> o (e t)"), reads=acc_r, writes=den_sr)
                dma("sp", rd, den_s.rearrange("o (p j) -> (o p) j", j=32), reads=den_sr, writes=rd_r)
                kb.op("dve", lambda e: e.reciprocal(out=rd, in_=rd), rd_r, rd_r)
                dma("sp", rden_s.rearrange("o (p j) -> (o p) j", j=32), rd, reads=rd_r, writes=rden_sr)
                dma("sp", rbc.rearrange("p e t -> p (e t)"), rden_s.to_broadcast([64, 2 * S]), reads=rden_sr, writes=rbc_r)

            def norm_mul(pp_):
                fns = []
                for e2 in range(2):
                    for tq in range(4):
                        cs = slice(tq * 512, (tq + 1) * 512)
                        fns.append((lambda cs, e2: lambda: kb.op("dve", lambda e: e.tensor_tensor(
                            out=attnT[64 * e2:64 * e2 + 64, pp_, cs], in0=acc[0:64, e2, cs], in1=rbc[:, e2, cs],
                            op=ALU.mult), [acc_r[e2]] + rbc_r, [attnT_r[pp_]]))(cs, e2))
                return fns

            pt_cnt = [0]
            pending_norm = None
            pairs = [(pp, g) for pp in range(4) for g in range(3)]

            def proj_groups(k):
                d = GROUPS[pairs[k][1]][1]
                wb = k % 2
                wqk_ring.need(k)

                def grp(which, tq):
                    bk, br = next_bank()
                    for dc in range(8):
                        mm(bk[:, :], wqk[:, wb, which, dc, :], xT[:, dc, tq * 512:(tq + 1) * 512],
                           dc == 0, dc == 7, xT_r + [wqk_r[wb]], [br])
                    dst = qk[:, wb, which, :].rearrange("p (r l) -> p r l", r=d)[:, :, tq * 512 // d:(tq + 1) * 512 // d]
                    src = bk[:, :].rearrange("p (m r) -> p r m", r=d)
                    evac_copy(dst, src, [br], [qk_r[wb * 2 + which]], scale=(0.125 if which == 0 else None))
                    bfree(br)

                return [(lambda which, tq: lambda: grp(which, tq))(which, tq) for which in range(2) for tq in range(4)]

            for fn_ in proj_groups(0):
                fn_()
            for pair_i, (pp, g) in enumerate(pairs):
                if True:
                    win, d = GROUPS[g]
                    L = S // d
                    nkc = L // 128
                    wb = pair_i % 2
                    def make_head(e2, g=g, d=d, L=L, nkc=nkc, pp=pp, wb=wb):
                        hg = 2 * pp + e2
                        h = 8 * g + hg
                        b0 = 64 * e2
                        qT = qk[b0:b0 + 64, wb, 0, :]
                        kT = qk[b0:b0 + 64, wb, 1, :]
                        qkres = [qk_r[wb * 2], qk_r[wb * 2 + 1]]
                        visits = []
                        for c in range(16):
                            seg, kc = c // nkc, c % nkc
                            qa, qb = max(0, 128 * kc - 64), min(L, 128 * kc + 192)
                            j0 = qa - (128 * kc - 64)
                            if d == 16:
                                bank_i, bcol = c // 4, 128 * (c % 4)
                            else:
                                bank_i, bcol = c // 2, 256 * (c % 2) + j0
                            visits.append((c, seg * L + qa, qb - qa, j0, bank_i, bcol))
                        nb = visits[-1][4] + 1
                        sbanks = {}
                        otile = {}

                        def emit_qk(bi):
                            bk, br = next_bank()
                            sbanks[bi] = (bk, br)
                            first = True
                            thunks = []
                            for (c, q0, w, j0, bank_i, bcol) in visits:
                                if bank_i != bi:
                                    continue
                                thunks.append((lambda c, q0, w, bcol, first: lambda: mm(
                                    bk[:, bcol:bcol + w], kT[:, c * 128:(c + 1) * 128], qT[:, q0:q0 + w],
                                    first, False, qkres, [br], skip=True))(c, q0, w, bcol, first))
                                first = False
                            return thunks

                        def emit_exp(bi, pb):
                            bk, br = sbanks.pop(bi)
                            vs = [v for v in visits if v[4] == bi]
                            c0 = min(v[5] for v in vs)
                            c1 = max(v[5] + v[2] for v in vs)
                            kb.op("act", lambda e: e.activation(out=PT[:, pb, c0:c1], in_=bk[:, c0:c1], func=AF.Exp),
                                  [br], [PT_r[pb]])
                            bfree(br)
                            tcols = tab[:, h, c0:c1]
                            kb.op("dve", lambda e: e.tensor_tensor(out=PT[:, pb, c0:c1], in0=PT[:, pb, c0:c1],
                                                                   in1=tcols, op=ALU.mult),
                                  [PT_r[pb]] + tab_r, [PT_r[pb]])

                        def emit_pv(bi, pb, pend):
                            vs = [v for v in visits if v[4] == bi]
                            for (c, q0, w, j0, bank_i, bcol) in vs:
                                pos = q0
                                while pos < q0 + w:
                                    ot = pos // 512
                                    end = min(q0 + w, (ot + 1) * 512)
                                    if ot not in otile:
                                        otile[ot] = next_bank() + (True,)
                                    ob, obr, first = otile[ot]
                                    otile[ot] = (ob, obr, False)
                                    mm(ob[0:65, pos - ot * 512:end - ot * 512], V[:, g, c, hg, :],
                                       PT[:, pb, bcol + pos - q0:bcol + end - q0],
                                       first, False, [V_r[g], PT_r[pb]], [obr], skip=True)
                                    pos = end
                            nxt = [v for v in visits if v[4] > bi]
                            lim = nxt[0][1] if nxt else S
                            for ot in sorted(otile):
                                if (ot + 1) * 512 <= lim:
                                    ob, obr, _f = otile.pop(ot)

                                    def merge(ot=ot, ob=ob, obr=obr):
                                        if g == 0:
                                            av = acc[:, e2, ot * 512:(ot + 1) * 512]
                                            kb.op("dve", lambda e: e.tensor_copy(out=av, in_=ob[0:65, :]),
                                                  [obr], [acc_r[e2]])
                                        elif g == 1:
                                            av = acc[:, e2, ot:S:4]
                                            kb.op("dve", lambda e: e.tensor_tensor(out=av, in0=ob[0:65, :], in1=av, op=ALU.add),
                                                  [obr, acc_r[e2]], [acc_r[e2]])
                                        else:
                                            av = acc[:, e2, :].rearrange("p (n r) -> p r n", r=16)[:, 4 * ot:4 * ot + 4, :]
                                            src = ob[0:65, :].rearrange("p (j n) -> p j n", j=4)
                                            kb.op("dve", lambda e: e.tensor_tensor(out=av, in0=src, in1=av, op=ALU.add),
                                                  [obr, acc_r[e2]], [acc_r[e2]])
                                        bfree(obr)

                                    pend.append(merge)
                            if not nxt:
                                assert not otile

                        return nb, emit_qk, emit_exp, emit_pv

                    heads = [make_head(0), make_head(1)]
                    items = [(hd, bi) for bi in range(heads[0][0]) for hd in heads]
                    LA = 1

                    def qk_pair(j):
                        if 2 * j >= len(items):
                            return
                        ta = items[2 * j][0][1](items[2 * j][1])
                        tb = items[2 * j + 1][0][1](items[2 * j + 1][1])
                        for x in range(max(len(ta), len(tb))):
                            if x < len(ta):
                                ta[x]()
                            if x < len(tb):
                                tb[x]()

                    for j in range(LA):
                        qk_pair(j)
                    pg = proj_groups(pair_i + 1) if pair_i + 1 < len(pairs) else []
                    if pg:
                        pg.pop(0)()
                    norm_fns = []
                    if pending_norm is not None:
                        norm_fns = norm_mul(pending_norm)
                        pending_norm = None
                    pend = []
                    pbs = {}

                    def do_exp(ii):
                        pbs[ii] = pt_cnt[0] % 4
                        pt_cnt[0] += 1
                        items[ii][0][2](items[ii][1], pbs[ii])

                    do_exp(0)
                    for ii, it in enumerate(items):
                        if ii % 2 == 0:
                            qk_pair(ii // 2 + LA)
                        if ii + 1 < len(items):
                            do_exp(ii + 1)
                        for _ in range(2):
                            if norm_fns:
                                norm_fns.pop(0)()
                        for mfn in pend:
                            mfn()
                        pend = []
                        it[0][3](it[1], pbs[ii], pend)
                        if ii % 2 == 1 and pg:
                            pg.pop(0)()
                    while pg:
                        pg.pop(0)()
                    for mfn in pend:
                        mfn()
                if g == 2:
                    norm_dma()
                    pending_norm = pp
            for fn_ in norm_mul(pending_norm):
                fn_()
            mem.release()
            if debug is not None and debug[0] == "attnT" and s == 0:
                finish_debug(attnT, attnT_r)

            reft, reft_r, _ = palloc("reft", BF16, [4, S], nres=4)
            wao, wao_r, _ = palloc("wao", BF16, [4, D])
            wfo, wfo_r, _ = palloc("wfo", BF16, [4, D])
            dma("pool", wao, wao_d, writes=wao_r)
            dma("pool", wfo, wfo_d, writes=wfo_r)
            mem.mark()
            UT, UT_r, _ = palloc("UT", BF16, [4, S], nres=16)
            UTe, UTe_r, _ = palloc("UTe", BF16, [4, S // 2])
            UTo, UTo_r, _ = palloc("UTo", BF16, [4, S // 2])
            AB, AB_r, _ = palloc("AB", BF16, [8, 4, 256], nres=1)
            a1024, a1024_r, _ = palloc("a1024", BF16, [4, 128], parts=1)
            c1024, c1024_r, _ = palloc("c1024", BF16, [S], parts=1)
            dftb, dftb_r, _ = palloc("dftb", BF16, [2, 2, 8, 512], nres=2)
            dma("pool", c1024, c1024_d, writes=c1024_r)
            dft_ring = Ring(4, 2, lambda k, b: dma("sp", dftb[:, b].rearrange("p a c n -> p (a c n)"), dft_s[k],
                                                   reads=[dft_sr[k]], writes=[dftb_r[b]]))
            dft_ring.need(0)
            for gq in range(4):
                for tq in range(4):
                    bk, br = next_bank()
                    for dc in range(8):
                        mm(bk[:, :], wf[:, dc, gq * 128:(gq + 1) * 128], xT[:, dc, tq * 512:(tq + 1) * 512],
                           dc == 0, dc == 7, xT_r + wf_r, [br])
                    evac_copy(UT[:, gq, tq * 512:(tq + 1) * 512], bk[:, :], [br], [UT_r[gq * 4 + tq]], eng="act")
                    bfree(br)
            H = S // 2
            kb.op("dve", lambda e: e.tensor_tensor(out=UTe[:, :, 1:H], in0=UT[:, :, 1:H], in1=UT[:, :, S - 1:H:-1],
                                                   op=ALU.add), UT_r, UTe_r)
            kb.op("dve", lambda e: e.tensor_copy(out=UTe[:, :, 0:1], in_=UT[:, :, 0:1]), UT_r, UTe_r)
            kb.op("dve", lambda e: e.tensor_tensor(out=UTo[:, :, 1:H], in0=UT[:, :, 1:H], in1=UT[:, :, S - 1:H:-1],
                                                   op=ALU.subtract), UT_r, UTo_r)
            kb.op("dve", lambda e: e.memset(UTo[:, :, 0:1], 0.0), (), UTo_r)
            bk, br = next_bank()
            for gq in range(4):
                mm(bk[0:1, gq * 128:(gq + 1) * 128], UT[:, gq, H:H + 1], ccsc[:, 0:128], True, True,
                   UT_r + ccsc_r, [br], skip=True)
            evac_copy(a1024, bk[0:1, :].rearrange("p (g n) -> p g n", g=4), [br], a1024_r, eng="act")
            bfree(br)
            for c in range(8):
                for gp in range(2):
                    bk, br = next_bank()
                    for j in range(2):
                        gq = gp * 2 + j
                        mm(bk[:, j * 256:j * 256 + 128], UTe[:, gq, c * 128:(c + 1) * 128], ccsc[:, 0:128], True, True,
                           UTe_r + ccsc_r, [br], skip=True)
                        mm(bk[:, j * 256 + 128:(j + 1) * 256], UTo[:, gq, c * 128:(c + 1) * 128], ccsc[:, 128:256],
                           True, True, UTo_r + ccsc_r, [br], skip=True)
                    evac_copy(AB[:, c, gp * 2:gp * 2 + 2, :], bk[:, :].rearrange("p (j n) -> p j n", j=2), [br], AB_r)
                    bfree(br)
            for kt in range(4):
                fb = [next_bank() for _ in range(4)]
                dft_ring.need(kt)
                b = kt % 2
                for cs_ in range(2):
                    for c in range(8):
                        for gq in range(4):
                            mm(fb[gq][0][:, :], AB[:, c, gq, cs_ * 128:(cs_ + 1) * 128], dftb[:, b, cs_, c, :],
                               cs_ == 0 and c == 0, False, AB_r + [dftb_r[b]], [fb[gq][1]])
                for gq in range(4):
                    mm(fb[gq][0][:, :], a1024[0:1, gq, :], c1024[0:1, kt * 512:(kt + 1) * 512], False, True,
                       a1024_r + c1024_r, [fb[gq][1]])
                for gq in range(4):
                    evac_copy(reft[:, gq, kt * 512:(kt + 1) * 512], fb[gq][0][:, :], [fb[gq][1]], [reft_r[gq]])
                    bfree(fb[gq][1])
            mem.release()
            if debug is not None and debug[0] == "reft" and s == 0:
                finish_debug(reft, reft_r)

            mergedT, mergedT_r, mergedT_off = palloc("mergedT", BF16, [8, S], nres=8, high=True)
            wout, wout_r, _ = palloc("wout", BF16, [8, D], high=True)
            wup, wup_r, _ = palloc("wup", BF16, [2, 8, 512], nres=2, high=True)
            dma("sp", wout, wout_s, reads=wout_sr, writes=wout_r)
            wup_ring = Ring(32, 2, lambda k, b: dma("sp", wup[:, b], wup_s[k % 8], reads=[wup_sr[k % 8]],
                                                    writes=[wup_r[b]]))
            wup_ring.need(0)
            mem.mark()
            wg, wg_r, _ = palloc("wg", BF16, [4, 8, 128], nres=4)
            gts, gts_r, _ = palloc("gts", BF16, [2, 2, 512], nres=4)
            t12, t12_r, _ = palloc("t12", F32, [2, 2, 512], nres=4)
            wg_ring = Ring(8, 2, lambda k, b: dma(
                "pool", wg[:, 2 * b:2 * b + 2],
                wg_d.rearrange("(a c) p k n -> c p a k n", a=2)[k], writes=[wg_r[2 * b], wg_r[2 * b + 1]]))
            for dc in range(8):
                wg_ring.need(dc)
                wslots = [2 * (dc % 2), 2 * (dc % 2) + 1]
                for tq in range(4):
                    cs = slice(tq * 512, (tq + 1) * 512)
                    gbuf = (dc * 4 + tq) % 2
                    for af in range(2):
                        bk, br = next_bank()
                        for kc in range(8):
                            mm(bk[:, :], wg[:, wslots[af], kc, :], xT[:, kc, cs], kc == 0, kc == 7,
                               xT_r + [wg_r[wslots[af]]], [br])
                        kb.op("act", (lambda bk, gbuf, af: lambda e: e.activation(
                            out=gts[:, gbuf, af, :], in_=bk[:, :], func=AF.Sigmoid))(bk, gbuf, af),
                            [br], [gts_r[gbuf * 2 + af]])
                        bfree(br)
                    bka, bra = next_bank()
                    for hp in range(4):
                        mm(bka[:, :], wao[:, hp, dc * 128:(dc + 1) * 128], attnT[:, hp, cs], hp == 0, hp == 3,
                           attnT_r + wao_r, [bra])
                    kb.op("dve", (lambda bka, gbuf: lambda e: e.tensor_tensor(
                        out=t12[:, gbuf, 0, :], in0=bka[:, :], in1=gts[:, gbuf, 0, :], op=ALU.mult))(bka, gbuf),
                        [bra, gts_r[gbuf * 2]], [t12_r[gbuf * 2]])
                    bfree(bra)
                    bkf, brf = next_bank()
                    for gq in range(4):
                        mm(bkf[:, :], wfo[:, gq, dc * 128:(dc + 1) * 128], reft[:, gq, cs], gq == 0, gq == 3,
                           reft_r + wfo_r, [brf])
                    kb.op("dve", (lambda bkf, gbuf: lambda e: e.tensor_tensor(
                        out=t12[:, gbuf, 1, :], in0=bkf[:, :], in1=gts[:, gbuf, 1, :], op=ALU.mult))(bkf, gbuf),
                        [brf, gts_r[gbuf * 2 + 1]], [t12_r[gbuf * 2 + 1]])
                    bfree(brf)
                    kb.op("dve", (lambda gbuf, dc, cs: lambda e: e.tensor_tensor(
                        out=mergedT[:, dc, cs], in0=t12[:, gbuf, 0, :], in1=t12[:, gbuf, 1, :], op=ALU.add))(gbuf, dc, cs),
                        [t12_r[gbuf * 2], t12_r[gbuf * 2 + 1]], [mergedT_r[dc]])
            mem.release()
            mem.release()
            if debug is not None and debug[0] == "mergedT" and s == 0:
                finish_debug(mergedT, mergedT_r)

            mem.mark()
            xr, xr_r, _ = palloc("xr", F32, [3, D], nres=3)
            z1, z1_r, _ = palloc("z1", F32, [2, D], nres=2)
            st, st_r, _ = palloc("st", F32, [4, 16], nres=4)
            ah1T, ah1T_r, _ = palloc("ah1T", F32, [2, 8, 512], nres=16)
            h1T, h1T_r, _ = palloc("h1T", BF16, [8, 512], nres=8)
            hidT, hidT_r, _ = palloc("hidT", BF16, [32, 512], nres=32)
            rr, rr_r, _ = palloc("rr", F32, [2, 512], nres=2)
            wdn, wdn_r, _ = palloc("wdn", BF16, [2, 32, 128], nres=2)
            zn2, zn2_r, zn2_off = palloc("zn2", F32, [3, D], nres=3)
            xr_ring = Ring(16, 3, lambda k, b: dma("sp", xr[:, b, :], x_d[s, k * 128:(k + 1) * 128, :],
                                                   writes=[xr_r[b]]))
            wdn_ring = Ring(32, 2, lambda k, b: dma("sp", wdn[:, b], wdn_s[k % 8], reads=[wdn_sr[k % 8]],
                                                    writes=[wdn_r[b]]))
            xr_ring.need(0)
            wdn_ring.need(0)
            cnt = {"z": 0, "st": 0, "zn": 0}
            stores = []

            def ln_stats(src0, src1, sb, reads):
                for half, src in enumerate((src0, src1)):
                    kb.op("dve", (lambda src, half: lambda e: e.bn_stats(
                        out=st[:, sb, half * 6:half * 6 + 6], in_=src))(src, half), reads[half], [st_r[sb]])
                kb.op("dve", lambda e: e.bn_aggr(
                    out=st[:, sb, 12:14], in_=st[:, sb, 0:12].rearrange("p (a b) -> p a b", a=2)),
                    [st_r[sb]], [st_r[sb]])
                kb.op("dve", lambda e: e.tensor_scalar(
                    out=st[:, sb, 13:14], in0=st[:, sb, 13:14], scalar1=EPS, scalar2=None, op0=ALU.add),
                    [st_r[sb]], [st_r[sb]])
                kb.op("pool", lambda e: e.tensor_tensor(
                    out=st[:, sb, 14:15], in0=st[:, sb, 13:14], in1=mhalf[:, 0:1], op=ALU.pow),
                    [st_r[sb]] + mhalf_r, [st_r[sb]])
                kb.op("dve", lambda e: e.scalar_tensor_tensor(
                    out=st[:, sb, 15:16], in0=st[:, sb, 12:13], scalar=-1.0, in1=st[:, sb, 14:15],
                    op0=ALU.mult, op1=ALU.mult), [st_r[sb]], [st_r[sb]])

            def ln1_mix(tq, sub):
                tt = tq * 4 + sub
                tok = slice(tt * 128, (tt + 1) * 128)
                xr_ring.need(tt)
                xb = tt % 3
                zb = cnt["z"] % 2
                cnt["z"] += 1
                sb = cnt["st"] % 4
                cnt["st"] += 1
                for half in range(2):
                    bk, br = next_bank()
                    for dc in range(8):
                        mm(bk[:, :], mergedT[:, dc, tok], wout[:, dc, half * 512:(half + 1) * 512], dc == 0, dc == 7,
                           mergedT_r + wout_r, [br])
                    kb.op("dve", (lambda bk, half: lambda e: e.scalar_tensor_tensor(
                        out=z1[:, zb, half * 512:(half + 1) * 512], in0=xr[:, xb, half * 512:(half + 1) * 512],
                        scalar=ALPHA, in1=bk[:, :], op0=ALU.mult, op1=ALU.add))(bk, half),
                        [br, xr_r[xb]], [z1_r[zb]])
                    bfree(br)
                ln_stats(z1[:, zb, 0:512], z1[:, zb, 512:1024], sb, [[z1_r[zb]], [z1_r[zb]]])
                kb.op("act", lambda e: e.activation(
                    out=z1[:, zb, :], in_=z1[:, zb, :], func=AF.Identity, scale=st[:, sb, 14:15],
                    bias=st[:, sb, 15:16]), [st_r[sb], z1_r[zb]], [z1_r[zb]])
                return zb

            def ln1_tr(tq, sub, zb):
                ab = tq % 2
                for half in range(2):
                    bk, br = next_bank()
                    for j in range(4):
                        dc = half * 4 + j
                        tr(bk[:, j * 128:(j + 1) * 128], z1[:, zb, dc * 128:(dc + 1) * 128], [z1_r[zb]], [br])
                    for j in range(4):
                        dc = half * 4 + j
                        for (dst, dst_r, so, bo) in ((h1T[:, dc, sub * 128:(sub + 1) * 128], h1T_r[dc], G1, B1),
                                                     (ah1T[:, ab, dc, sub * 128:(sub + 1) * 128], ah1T_r[ab * 8 + dc], AG1, AB1)):
                            if half == 0:
                                kb.op("act", (lambda bk, j, dc, dst, so, bo: lambda e: e.activation(
                                    out=dst, in_=bk[:, j * 128:(j + 1) * 128],
                                    func=AF.Identity, scale=pvec[:, so + dc:so + dc + 1],
                                    bias=pvec[:, bo + dc:bo + dc + 1]))(bk, j, dc, dst, so, bo),
                                    [br] + pvec_r, [dst_r])
                            else:
                                kb.op("dve", (lambda bk, j, dc, dst, so, bo: lambda e: e.tensor_scalar(
                                    out=dst, in0=bk[:, j * 128:(j + 1) * 128],
                                    scalar1=pvec[:, so + dc:so + dc + 1], scalar2=pvec[:, bo + dc:bo + dc + 1],
                                    op0=ALU.mult, op1=ALU.add))(bk, j, dc, dst, so, bo),
                                    [br] + pvec_r, [dst_r])
                    bfree(br)

            def ln2(tq):
                for sub in range(4):
                    ln2_sub(tq, sub)

            def ln2_sub(tq, sub):
                ab = tq % 2
                if True:
                    tt = tq * 4 + sub
                    tok = slice(tt * 128, (tt + 1) * 128)
                    zb = cnt["zn"] % 3
                    cnt["zn"] += 1
                    sb = cnt["st"] % 4
                    cnt["st"] += 1
                    hb = []
                    for half in range(2):
                        bk, br = next_bank()
                        hb.append((bk, br))
                        for j in range(4):
                            dc = half * 4 + j
                            tr(bk[:, j * 128:(j + 1) * 128], ah1T[:, ab, dc, sub * 128:(sub + 1) * 128],
                               [ah1T_r[ab * 8 + dc]], [br])
                    ln_stats(hb[0][0][:, :], hb[1][0][:, :], sb, [[hb[0][1]], [hb[1][1]]])
                    bk, br = hb[0]
                    kb.op("dve", (lambda bk: lambda e: e.scalar_tensor_tensor(
                        out=zn2[:, zb, 0:512], in0=bk[:, :], scalar=st[:, sb, 12:13], in1=g2b[:, 0:512],
                        op0=ALU.subtract, op1=ALU.mult))(bk), [br, st_r[sb]] + g2b_r, [zn2_r[zb]])
                    bfree(br)
                    kb.op("dve", lambda e: e.scalar_tensor_tensor(
                        out=zn2[:, zb, 0:512], in0=zn2[:, zb, 0:512], scalar=st[:, sb, 14:15], in1=b2b[:, 0:512],
                        op0=ALU.mult, op1=ALU.add), [zn2_r[zb], st_r[sb]] + b2b_r, [zn2_r[zb]])
                    bk, br = hb[1]
                    kb.op("act", (lambda bk: lambda e: e.activation(
                        out=zn2[:, zb, 512:1024], in_=bk[:, :], func=AF.Identity,
                        scale=st[:, sb, 14:15], bias=st[:, sb, 15:16]))(bk), [br, st_r[sb]], [zn2_r[zb]])
                    bfree(br)
                    kb.op("pool", lambda e: e.tensor_tensor(
                        out=zn2[:, zb, 512:1024], in0=zn2[:, zb, 512:1024], in1=g2b[:, 512:1024], op=ALU.mult),
                        [zn2_r[zb]] + g2b_r, [zn2_r[zb]])
                    kb.op("pool", lambda e: e.tensor_tensor(
                        out=zn2[:, zb, 512:1024], in0=zn2[:, zb, 512:1024], in1=b2b[:, 512:1024], op=ALU.add),
                        [zn2_r[zb]] + b2b_r, [zn2_r[zb]])
                    stores.append((lambda dst, srcap, rs: lambda: dma("sp", dst, srcap, reads=rs, is_out=True))(
                        out_d[s, tok, :], zn2[:, zb, :], [zn2_r[zb]]))

            def up_fg(tq, fg):
                k = tq * 8 + fg
                wup_ring.need(k)
                ub = k % 2
                for j in range(4):
                    fc = fg * 4 + j
                    bk, br = next_bank()
                    for dc in range(8):
                        mm(bk[:, :], wup[:, ub, dc, j * 128:(j + 1) * 128], h1T[:, dc, :], dc == 0, dc == 7,
                           h1T_r + [wup_r[ub]], [br])
                    rb = fc % 2
                    kb.op("dve", (lambda bk, rb, fc: lambda e: e.tensor_scalar(
                        out=rr[:, rb, :], in0=bk[:, :], scalar1=pvec[:, BUP + fc:BUP + fc + 1], scalar2=0.0,
                        op0=ALU.add, op1=ALU.max))(bk, rb, fc), [br] + pvec_r, [rr_r[rb]])
                    bfree(br)
                    kb.op("act", (lambda rb, fc: lambda e: e.activation(
                        out=hidT[:, fc, :], in_=rr[:, rb, :], func=AF.Square))(rb, fc), [rr_r[rb]], [hidT_r[fc]])

            def down_dc(tq, dc):
                ab = tq % 2
                k = tq * 8 + dc
                wdn_ring.need(k)
                db = k % 2
                bk, br = next_bank()
                for fc in range(32):
                    mm(bk[:, :], wdn[:, db, fc, :], hidT[:, fc, :], fc == 0, fc == 31, hidT_r + [wdn_r[db]], [br])
                kb.op("dve", (lambda bk, dc, ab: lambda e: e.scalar_tensor_tensor(
                    out=ah1T[:, ab, dc, :], in0=bk[:, :], scalar=pvec[:, BDN + dc:BDN + dc + 1], in1=ah1T[:, ab, dc, :],
                    op0=ALU.add, op1=ALU.add))(bk, dc, ab), [br, ah1T_r[ab * 8 + dc]] + pvec_r, [ah1T_r[ab * 8 + dc]])
                bfree(br)

            pend = None
            for sub in range(4):
                zb = ln1_mix(0, sub)
                if pend is not None:
                    ln1_tr(0, pend[0], pend[1])
                pend = (sub, zb)
            ln1_tr(0, pend[0], pend[1])
            if debug is not None and debug[0] == "h1T":
                finish_debug(h1T, h1T_r)
            for tq in range(4):
                for fg in range(8):
                    up_fg(tq, fg)
                    if tq > 0 and 2 <= fg < 6:
                        stores.pop(0)()
                    if tq > 0 and fg < 4:
                        ln2_sub(tq - 1, fg)
                zbs = {}
                for dc in range(8):
                    down_dc(tq, dc)
                    if tq < 3:
                        if 1 <= dc <= 4:
                            ln1_tr(tq + 1, dc - 1, zbs[dc - 1])
                        if dc < 4:
                            zbs[dc] = ln1_mix(tq + 1, dc)
            for sub in range(4):
                ln2_sub(3, sub)
                if len(stores) > 1:
                    stores.pop(0)()
            if s + 1 < nseq:
                late_stores.extend([list(stores), zn2_off, zn2_off + 3 * D * 4])
                stores = []
            while stores:
                stores.pop(0)()
            mem.release()
            mem.release()
            for reg in mem.regions:
                if reg[0] >= mem.htop:
                    reg[3] = False
            mem.htop = POOL_BYTES
    except _Stop:
        pass

    kb.emit()
    return nc


def host_inputs(x, rel_bias, w_in, w_fourier_out, w_attn_out, w_out, ln1_g, ln1_b, w_up, b_up, w_down, b_down,
                ln2_g, ln2_b):
    f32 = np.float32
    w_in = np.asarray(w_in, f32)[0]
    Q, K_, V_, F_, GA, GF = 0, 1536, 3072, 4608, 5120, 6144

    def tile_w(w):
        return np.ascontiguousarray(w.reshape(8, 128, -1).transpose(1, 0, 2))

    wqk = np.stack([np.stack([tile_w(w_in[:, Q + 128 * p:Q + 128 * (p + 1)]),
                              tile_w(w_in[:, K_ + 128 * p:K_ + 128 * (p + 1)])]) for p in range(12)])
    wv = np.stack([tile_w(w_in[:, V_ + 512 * g:V_ + 512 * (g + 1)]) for g in range(3)])
    wf = tile_w(w_in[:, F_:F_ + 512])
    wg = np.stack([tile_w(w_in[:, GA + 128 * i:GA + 128 * (i + 1)]) for i in range(16)])
    wao = np.ascontiguousarray(np.asarray(w_attn_out, f32)[0].reshape(4, 128, D).transpose(1, 0, 2))
    wfo = np.ascontiguousarray(np.asarray(w_fourier_out, f32)[0].reshape(4, 128, D).transpose(1, 0, 2))
    wout = tile_w(np.asarray(w_out, f32)[0])
    wu = np.asarray(w_up, f32)[0]
    wup = np.stack([tile_w(wu[:, 512 * i:512 * (i + 1)]) for i in range(8)])
    wd = np.asarray(w_down, f32)[0]
    wdn = np.stack([np.ascontiguousarray(wd[:, 128 * dc:128 * (dc + 1)].reshape(32, 128, 128).transpose(1, 0, 2))
                    for dc in range(8)])
    rb = np.asarray(rel_bias, f32)
    i = np.arange(128)[:, None]
    j = np.arange(256)[None, :]
    rel = i - j + 64
    valid = np.abs(rel) <= 64
    tab = np.empty((128, 24, 256), f32)
    for g, (win, d) in enumerate(GROUPS):
        bk = t5_buckets(rel * d)
        for hh in range(8):
            h = 8 * g + hh
            tab[:, h, :] = np.where(valid, rb[bk, h], f32(NEG))
    n = np.arange(S, dtype=np.int64)
    ang = 2.0 * np.pi * ((n[:, None] * n[None, :]) % S).astype(np.float64) / S
    cs = np.stack([np.cos(ang), np.sin(ang)]) / np.sqrt(S)
    c1024 = np.ascontiguousarray(cs[0, S // 2:S // 2 + 1, :]).astype(f32)
    dft = cs[:, :S // 2].reshape(2, 8, 128, 4, 512).transpose(3, 2, 0, 1, 4)
    dft = np.ascontiguousarray(dft).astype(f32)
    m = np.arange(128, dtype=np.int64)
    a2 = 2.0 * np.pi * ((m[:, None] * m[None, :]) % 128).astype(np.float64) / 128
    ccsc = (np.concatenate([np.cos(a2), -np.sin(a2)], axis=1) / np.sqrt(128.0)).astype(f32)
    identf = np.eye(128, dtype=f32)
    identb = np.eye(128).astype(ml_dtypes.bfloat16)

    def pp(v):
        return np.asarray(v, f32).reshape(-1, 128).T

    g1, b1 = np.asarray(ln1_g, f32)[0], np.asarray(ln1_b, f32)[0]
    pvec_parts = [pp(g1), pp(b1), None, None, pp(np.asarray(b_down, f32)[0]), pp(np.asarray(b_up, f32)[0])]
    shared = dict(wqk=wqk, wv=wv, wf=wf, wg=wg, wao=wao, wfo=wfo, wout=wout, wup=wup, wdn=wdn, tab=tab, dft=dft,
                  ccsc=ccsc, c1024=c1024, identf=identf,
                  dft_s=np.zeros((4, 128, 2 * 8 * 512), ml_dtypes.bfloat16), identb=identb,
                  g2b=np.ascontiguousarray(np.broadcast_to(np.asarray(ln2_g, f32)[0], (128, D))),
                  b2b=np.ascontiguousarray(np.broadcast_to(np.asarray(ln2_b, f32)[0], (128, D))))
    return shared, pvec_parts


_CACHE = {}


def kernel(x, rel_bias, w_in, w_fourier_out, w_attn_out, w_out, ln1_g, ln1_b, w_up, b_up, w_down, b_down,
           ln2_g, ln2_b, _debug=None, _ncores=NCORES):
    x = np.asarray(x, np.float32)
    shared, pv = host_inputs(x, rel_bias, w_in, w_fourier_out, w_attn_out, w_out, ln1_g, ln1_b, w_up, b_up,
                             w_down, b_down, ln2_g, ln2_b)
    pvec = np.zeros((128, 72), np.float32)
    pvec[:, 0:8] = pv[0]
    pvec[:, 8:16] = pv[1]
    pvec[:, 32:40] = pv[4]
    pvec[:, 40:72] = pv[5]
    shared["pvec"] = pvec
    nc = build(debug=_debug)
    in_maps = []
    for c in range(_ncores):
        m = dict(shared)
        m["x"] = np.ascontiguousarray(x[2 * c:2 * c + 2])
        in_maps.append(m)
    res = run_bass_kernel_spmd(nc, in_maps, core_ids=list(range(_ncores)))
    if _debug is not None:
        return res.results
    return np.concatenate([r["out"] for r in res.results], axis=0)
```

```python
import numpy as np
import ml_dtypes
import concourse.bass as bass
import concourse.mybir as mybir
from concourse.bass_utils import run_bass_kernel_spmd

F32 = mybir.dt.float32
BF16 = mybir.dt.bfloat16
AF = mybir.ActivationFunctionType
ALU = mybir.AluOpType

NCORES = 8
S = 2048
D = 1024
NSEQ = 2
DFF = 4096
ALPHA = 2.0 ** 0.25
EPS = 1e-5
NEG = -30000.0
GROUPS = ((128, 1), (512, 4), (2048, 16))
POOL_BYTES = 206 * 1024


class Res:
    __slots__ = ("name", "writers", "readers", "inherit", "excl")

    def __init__(self, name, inherit=(), excl=False):
        self.name = name
        self.excl = excl
        self.writers = []
        self.readers = []
        self.inherit = list(inherit)


class Op:
    __slots__ = ("eng", "fn", "deps", "dma", "sem", "val", "signal", "idx")


class KB:
    ENGS = ("pe", "act", "dve", "pool", "sp")

    def __init__(self, nc):
        self.nc = nc
        self.ops = {e: [] for e in self.ENGS}
        self.n = 0
        self.ndma = 0
        self.NDMASEM = 12
        self.nsw = 0
        self.dma_last = [None] * self.NDMASEM
        self.dma_cnt = [0] * self.NDMASEM
        self.out_dmas = []

    def op(self, eng, fn, reads=(), writes=(), dma=False, is_out=False):
        o = Op()
        o.eng, o.fn, o.dma, o.signal, o.idx = eng, fn, dma, False, self.n
        self.n += 1
        deps = []
        for r in reads:
            deps += r.inherit
            deps += r.writers
            if r.excl:
                deps += [q for q in r.readers if q.eng != eng]
        for r in writes:
            deps += r.inherit
            deps += r.writers
            deps += r.readers
        if dma and eng == "pool":
            o.sem, o.val = ("sw", self.nsw), 16
            self.nsw += 1
            o.signal = True
        elif dma:
            j = self.ndma % self.NDMASEM
            self.ndma += 1
            if self.dma_last[j] is not None:
                deps.append(self.dma_last[j])
            self.dma_cnt[j] += 1
            o.sem, o.val = j, 16 * self.dma_cnt[j]
            self.dma_last[j] = o
            o.signal = True
        else:
            o.sem, o.val = None, None
        seen = set()
        o.deps = []
        for d in deps:
            if d.idx in seen or d is o:
                continue
            seen.add(d.idx)
            if d.eng == "pe" and eng == "pe" and not d.dma:
                continue
            o.deps.append(d)
            d.signal = True
        for r in reads:
            r.readers.append(o)
        for r in writes:
            r.writers = [o]
            r.readers = []
            r.inherit = []
        self.ops[eng].append(o)
        if is_out:
            self.out_dmas.append(o)
        return o

    def emit(self):
        nc = self.nc
        esem = {e: nc.alloc_semaphore("s_" + e) for e in self.ENGS}
        dsem = {j: nc.alloc_semaphore("d_%d" % j) for j in range(self.NDMASEM)}
        for j in range(self.nsw):
            dsem[("sw", j)] = nc.alloc_semaphore("w_%d" % j)
        for e in self.ENGS:
            t = 0
            for o in self.ops[e]:
                if o.dma:
                    continue
                if o.signal:
                    t += 1
                    o.val = t
        final_waits = [(dsem[o.sem], o.val) for o in self.out_dmas]

        def run(e, eng):
            waited = {}
            for o in self.ops[e]:
                for d in o.deps:
                    if d.dma:
                        key, sem, val = ("d", d.sem), dsem[d.sem], d.val
                    else:
                        key, sem, val = ("e", d.eng), esem[d.eng], d.val
                    if waited.get(key, 0) >= val:
                        continue
                    waited[key] = val
                    eng.wait_ge(sem, val)
                ins = o.fn(eng)
                if o.dma:
                    ins.then_inc(dsem[o.sem], 16)
                elif o.signal:
                    ins.then_inc(esem[e], 1)
            if e == "sp":
                for sem, val in final_waits:
                    eng.wait_ge(sem, val)

        with nc.Block() as block:
            @block.sync
            def _(eng):
                run("sp", eng)

            @block.scalar
            def _(eng):
                run("act", eng)

            @block.vector
            def _(eng):
                run("dve", eng)

            @block.gpsimd
            def _(eng):
                run("pool", eng)

            @block.tensor
            def _(eng):
                run("pe", eng)


class Mem:
    def __init__(self, nc):
        self.pool = nc.alloc_sbuf_tensor("pool", [128, POOL_BYTES // 4], F32)
        self.regions = []
        self.top = 0
        self.htop = POOL_BYTES
        self.marks = []

    def mark(self):
        self.marks.append((self.top, len(self.regions)))

    def release(self):
        top, n = self.marks.pop()
        for i in range(n, len(self.regions)):
            self.regions[i][3] = False
        self.top = top

    def alloc_over(self, name, start, nbytes, nres=1):
        inherit = []
        for (s, e, rs, alive) in self.regions:
            if s < start + nbytes and start < e:
                for r in rs:
                    inherit += r.writers + r.readers + r.inherit
        rs = [Res("%s_%d" % (name, i), inherit) for i in range(nres)]
        self.regions.append([start, start + nbytes, rs, True])
        return rs

    def alloc(self, name, nbytes, nres=1, high=False, preset=None):
        nbytes = (nbytes + 31) // 32 * 32
        if high:
            self.htop -= nbytes
            start = self.htop
        else:
            start = self.top
            self.top += nbytes
        assert self.top <= self.htop, (name, self.top, self.htop, nbytes)
        inherit = []
        for (s, e, rs, alive) in self.regions:
            if (not alive) and s < start + nbytes and start < e:
                for r in rs:
                    inherit += r.writers + r.readers + r.inherit
        rs = preset if preset is not None else [Res("%s_%d" % (name, i), inherit) for i in range(nres)]
        self.regions.append([start, start + nbytes, rs, True])
        return start, rs

    def add_late(self, ops, start, end):
        for (s, e, rs, alive) in self.regions:
            if alive and s < end and start < e:
                for r in rs:
                    r.inherit += ops

    def view(self, off, dtype, shape, parts=128, p0=0):
        n = 1
        for s_ in shape:
            n *= s_
        assert off % 4 == 0
        if dtype == F32:
            ap = self.pool[p0:p0 + parts, off // 4: off // 4 + n]
        else:
            assert n % 2 == 0
            ap = self.pool[p0:p0 + parts, off // 4: off // 4 + n // 2].bitcast(BF16)
        if len(shape) == 1:
            return ap
        names = ["a%d" % i for i in range(len(shape))]
        pat = "p (%s) -> p %s" % (" ".join(names), " ".join(names))
        kw = {names[i]: shape[i] for i in range(1, len(shape))}
        return ap.rearrange(pat, **kw)


def t5_buckets(rel):
    half = 16
    ret = np.where(rel > 0, half, 0)
    n = np.abs(rel)
    max_exact = half // 2
    n_f = np.maximum(n, 1).astype(np.float32)
    large = max_exact + (np.log(n_f / max_exact) / np.log(1024 / max_exact) * (half - max_exact)).astype(np.int32)
    large = np.minimum(large, half - 1)
    return (ret + np.where(n < max_exact, n, large)).astype(np.int32)


def build(debug=None, nseq=NSEQ):
    nc = bass.Bass("TRN2", target_bir_lowering=False)
    kb = KB(nc)
    mem = Mem(nc)

    def din(name, shape, dt=F32):
        return nc.dram_tensor(name, list(shape), dt, kind="ExternalInput").ap()

    x_d = din("x", [NSEQ, S, D])
    wqk_d = din("wqk", [12, 2, 128, 8, 128])
    wv_d = din("wv", [3, 128, 8, 512])
    wf_d = din("wf", [128, 8, 512])
    wg_d = din("wg", [16, 128, 8, 128])
    wao_d = din("wao", [128, 4, D])
    wfo_d = din("wfo", [128, 4, D])
    wout_d = din("wout", [128, 8, D])
    wup_d = din("wup", [8, 128, 8, 512])
    wdn_d = din("wdn", [8, 128, 32, 128])
    tab_d = din("tab", [128, 24, 256])
    dft_d = din("dft", [4, 128, 2, 8, 512])
    c1024_d = din("c1024", [1, S])
    ccsc_d = din("ccsc", [128, 256])
    identf_d = din("identf", [128, 128])
    identb_d = din("identb", [128, 128], BF16)
    pvec_d = din("pvec", [128, 72])
    g2b_d = din("g2b", [128, D])
    b2b_d = din("b2b", [128, D])
    out_d = nc.dram_tensor("out", [NSEQ, S, D], F32, kind="ExternalOutput").ap()
    dbg_d = None
    if debug is not None:
        dbg_d = nc.dram_tensor("dbg", list(debug[1]), debug[2], kind="ExternalOutput").ap()

    banks = [nc.alloc_psum_tensor("bank%d" % i, [128, 512], F32) for i in range(8)]
    bres = [Res("bank%d" % i, excl=True) for i in range(8)]
    bank_free = list(range(8))

    def next_bank():
        i = bank_free.pop(0)
        return banks[i], bres[i]

    def bfree(br):
        i = bres.index(br)
        assert i not in bank_free
        bank_free.append(i)

    def palloc(name, dtype, shape, parts=128, nres=1, high=False, preset=None):
        n = 1
        for s_ in shape:
            n *= s_
        off, rs = mem.alloc(name, n * (4 if dtype == F32 else 2), nres, high, preset)
        return mem.view(off, dtype, shape, parts), rs, off

    identf, identf_r, _ = palloc("identf", F32, [128])
    identb, identb_r, _ = palloc("identb", BF16, [128])
    onesf, onesf_r, _ = palloc("onesf", F32, [64])
    ccsc, ccsc_r, _ = palloc("ccsc", BF16, [256])
    pvec, pvec_r, _ = palloc("pvec", F32, [72])
    g2b, g2b_r, _ = palloc("g2b", F32, [D])
    b2b, b2b_r, _ = palloc("b2b", F32, [D])
    mhalf, mhalf_r, _ = palloc("mhalf", F32, [8])

    def dma(eng, out, in_, reads=(), writes=(), is_out=False):
        return kb.op(eng, lambda e: e.dma_start(out=out, in_=in_), reads, writes, dma=True, is_out=is_out)

    dma("sp", identf, identf_d, writes=identf_r)
    dma("sp", identb, identb_d, writes=identb_r)
    dma("pool", ccsc, ccsc_d, writes=ccsc_r)
    dma("sp", pvec, pvec_d, writes=pvec_r)
    dma("sp", g2b, g2b_d, writes=g2b_r)
    dma("sp", b2b, b2b_d, writes=b2b_r)
    kb.op("dve", lambda e: e.memset(onesf, 1.0), writes=onesf_r)
    kb.op("dve", lambda e: e.memset(mhalf, -0.5), writes=mhalf_r)
    G1, B1, AG1, AB1, BDN, BUP = 0, 8, 16, 24, 32, 40
    kb.op("dve", lambda e: e.tensor_scalar(out=pvec[:, 16:32], in0=pvec[:, 0:16], scalar1=ALPHA, scalar2=None,
                                           op0=ALU.mult), pvec_r, pvec_r)

    evac_rr = [0]

    def evac_copy(out, in_, reads, writes, scale=None, eng=None):
        if eng is None:
            eng = ("act", "dve")[evac_rr[0] % 2]
            evac_rr[0] += 1
        if eng == "act":
            if scale is None:
                return kb.op("act", lambda e: e.activation(out=out, in_=in_, func=AF.Copy), reads, writes)
            return kb.op("act", lambda e: e.activation(out=out, in_=in_, func=AF.Copy, scale=scale), reads, writes)
        if scale is None:
            return kb.op("dve", lambda e: e.tensor_copy(out=out, in_=in_), reads, writes)
        return kb.op("dve", lambda e: e.tensor_scalar(out=out, in0=in_, scalar1=scale, scalar2=None, op0=ALU.mult),
                     reads, writes)

    def mm(out, lhsT, rhs, start, stop, reads, writes, skip=False):
        if skip:
            return kb.op("pe", lambda e: e.matmul(out, lhsT, rhs, start=start, stop=stop, skip_group_check=True),
                         reads, writes)
        return kb.op("pe", lambda e: e.matmul(out, lhsT, rhs, start=start, stop=stop), reads, writes)

    def tr(out, in_, reads, writes):
        return kb.op("pe", lambda e: e.transpose(out, in_, identf), list(reads) + identf_r, writes)


    class Ring:
        def __init__(self, n, nbuf, load):
            self.n, self.nbuf, self.load, self.nxt = n, nbuf, load, 0

        def need(self, k):
            lim = min(self.n - 1, k + self.nbuf - 1)
            while self.nxt <= lim:
                self.load(self.nxt, self.nxt % self.nbuf)
                self.nxt += 1

    wout_s = nc.dram_tensor("wout_s", [128, 8 * D], BF16).ap()
    wup_s = nc.dram_tensor("wup_s", [8, 128, 8 * 512], BF16).ap()
    wdn_s = nc.dram_tensor("wdn_s", [8, 128, 32 * 128], BF16).ap()
    wout_sr = [Res("wout_s")]
    wup_sr = [Res("wup_s%d" % i) for i in range(8)]
    wdn_sr = [Res("wdn_s%d" % i) for i in range(8)]

    cast_jobs = [(wout_s, wout_d.rearrange("p c n -> p (c n)"), wout_sr)]
    for i in range(8):
        cast_jobs.append((wup_s[i], wup_d[i].rearrange("p c n -> p (c n)"), [wup_sr[i]]))
    for i in range(8):
        cast_jobs.append((wdn_s[i], wdn_d[i].rearrange("p c n -> p (c n)"), [wdn_sr[i]]))
    dft_s = din("dft_s", [4, 128, 2 * 8 * 512], BF16)
    dft_sr = [Res("dft_s%d" % i) for i in range(4)]
    for i in range(4):
        cast_jobs.insert(1 + 2 * i, (dft_s[i], dft_d[i].rearrange("p a c n -> p (a c n)"), [dft_sr[i]]))

    def emit_scratch_casts(n):
        for _ in range(n):
            if cast_jobs:
                o_, i_, r_ = cast_jobs.pop(0)
                dma("pool", o_, i_, writes=r_)

    den_s = nc.dram_tensor("den_s", [1, 2 * S], F32).ap()
    rden_s = nc.dram_tensor("rden_s", [1, 2 * S], F32).ap()
    den_sr = [Res("den_s")]
    rden_sr = [Res("rden_s")]

    class _Stop(Exception):
        pass

    def finish_debug(ap, rs):
        dma("sp", dbg_d, ap, reads=rs, is_out=True)
        raise _Stop()

    late_stores = []
    pre = {}

    def phase0_tile(sq, tt, xT_ap, xT_rs, xs_ap, xs_rs, ring):
        ring.need(tt)
        b = tt % 3
        for half in range(2):
            bk, br = next_bank()
            for j in range(4):
                dc = half * 4 + j
                tr(bk[:, j * 128:(j + 1) * 128], xs_ap[:, b, dc * 128:(dc + 1) * 128], [xs_rs[b]], [br])
            evac_copy(xT_ap[:, half * 4:half * 4 + 4, tt * 128:(tt + 1) * 128],
                      bk[:, :].rearrange("p (j t) -> p j t", j=4), [br], [xT_rs[tt]])
            bfree(br)

    try:
        for s in range(nseq):
            mem.mark()
            mem.mark()
            pre_xT = pre.pop("xT", None)
            xT, xT_r, xT_off = palloc("xT", BF16, [8, S], nres=16, preset=pre_xT)
            attnT, attnT_r, _ = palloc("attnT", BF16, [4, S], nres=4)
            wf, wf_r, _ = palloc("wf", BF16, [8, 512], nres=1)

            if pre_xT is None:
                xs, xs_r, _ = palloc("xs", F32, [3, D], nres=3, high=True)
                xs_ring = Ring(16, 3, lambda k, b: dma("sp", xs[:, b, :], x_d[s, k * 128:(k + 1) * 128, :],
                                                       writes=[xs_r[b]]))
                xs_ring.need(0)
            if late_stores:
                n0 = kb.n
                for fn_ in late_stores[0]:
                    fn_()
                late_ops = [o for o in kb.ops["sp"] if o.idx >= n0]
                mem.add_late(late_ops, late_stores[1], late_stores[2])
                late_stores.clear()
            mem.mark()
            tab, tab_r, _ = palloc("tab", BF16, [24, 512])
            V, V_r, _ = palloc("V", BF16, [3, 16, 8, 65], nres=3)
            Vm = V.rearrange("p g c h e -> p (g c h) e")[:, :, 64:65]
            kb.op("dve", (lambda Vm: lambda e: e.memset(Vm, 1.0))(Vm), writes=V_r)
            mem.mark()
            wv, wv_r, _ = palloc("wv", BF16, [2, 8, 512], nres=2)
            wv_ring = Ring(3, 2, lambda k, b: dma("pool", wv[:, b], wv_d[k], writes=[wv_r[b]]))
            wv_ring.need(0)
            tabraw, tabraw_r, _ = palloc("tabraw", BF16, [24, 256])

            def v_chunk(g, d, nkc, c):
                seg, kc = c // nkc, c % nkc
                t0 = (128 * kc) * d + seg
                bk, br = next_bank()
                for dc in range(8):
                    if d == 1:
                        lhsT = xT[:, dc, t0:t0 + 128]
                    else:
                        lhsT = xT[:, dc, t0:t0 + 127 * d + 1:d]
                    mm(bk[:, :], lhsT, wv[:, g % 2, dc, :], dc == 0, dc == 7, xT_r + [wv_r[g % 2]], [br])
                evac_copy(V[:, g, c, :, 0:64], bk[:, :].rearrange("p (h e) -> p h e", h=8), [br], [V_r[g]])
                bfree(br)

            for tt in range(16):
                if pre_xT is None:
                    phase0_tile(s, tt, xT, xT_r, xs, xs_r, xs_ring)
                if tt >= 1:
                    v_chunk(0, 1, 16, tt - 1)
            v_chunk(0, 1, 16, 15)
            if debug is not None and debug[0] == "xT" and s == 0:
                finish_debug(xT, xT_r)
            dma("pool", tabraw, tab_d, writes=tabraw_r)
            for i in range(2):
                kb.op("act", (lambda i: lambda e: e.activation(out=tab[:, 0:16, 256 * i:256 * i + 256],
                                                                in_=tabraw[:, 0:16, :], func=AF.Exp))(i),
                      tabraw_r, tab_r)
            for i in range(4):
                kb.op("act", (lambda i: lambda e: e.activation(out=tab[:, 16:24, 128 * i:128 * i + 128],
                                                                in_=tabraw[:, 16:24, 64:192], func=AF.Exp))(i),
                      tabraw_r, tab_r)
            for g, (win, d) in enumerate(GROUPS):
                if g == 0:
                    continue
                L = S // d
                nkc = L // 128
                wv_ring.need(g)
                for c in range(16):
                    v_chunk(g, d, nkc, c)
            for reg in mem.regions:
                if reg[0] >= mem.htop:
                    reg[3] = False
            mem.htop = POOL_BYTES
            mem.release()
            dma("pool", wf, wf_d, writes=wf_r)
            acc, acc_r, _ = palloc("acc", F32, [2, S], parts=65, nres=2)
            qk, qk_r, _ = palloc("qk", BF16, [2, 2, S], nres=4)
            PT, PT_r, _ = palloc("PT", BF16, [4, 512], nres=4)
            wqk, wqk_r, _ = palloc("wqk", BF16, [2, 2, 8, 128], nres=2)
            pair_order = [4 * g + pp for pp in range(4) for g in range(3)]
            def load_wqk(k, b):
                dma("pool", wqk[:, b], wqk_d[pair_order[k]].rearrange("a p c n -> p a c n"), writes=[wqk_r[b]])
                if k >= 1:
                    emit_scratch_casts(2)

            wqk_ring = Ring(12, 2, load_wqk)
            rd, rd_r, _ = palloc("rd", F32, [32])
            rbc, rbc_r, _ = palloc("rbc", F32, [2, S], parts=64)

            def norm_dma():
                dma("sp", den_s, acc[64:65, :, :].rearrange("o e t -> o (e t)"), reads=acc_r, writes=den_sr)
                dma("sp", rd, den_s.rearrange("o (p j) -> (o p) j", j=32), reads=den_sr, writes=rd_r)
                kb.op("dve", lambda e: e.reciprocal(out=rd, in_=rd), rd_r, rd_r)
                dma("sp", rden_s.rearrange("o (p j) -> (o p) j", j=32), rd, reads=rd_r, writes=rden_sr)
                dma("sp", rbc.rearrange("p e t -> p (e t)"), rden_s.to_broadcast([64, 2 * S]), reads=rden_sr, writes=rbc_r)

            def norm_mul(pp_):
                fns = []
                for e2 in range(2):
                    for tq in range(4):
                        cs = slice(tq * 512, (tq + 1) * 512)
                        fns.append((lambda cs, e2: lambda: kb.op("dve", lambda e: e.tensor_tensor(
                            out=attnT[64 * e2:64 * e2 + 64, pp_, cs], in0=acc[0:64, e2, cs], in1=rbc[:, e2, cs],
                            op=ALU.mult), [acc_r[e2]] + rbc_r, [attnT_r[pp_]]))(cs, e2))
                return fns

            pt_cnt = [0]
            pending_norm = None
            pairs = [(pp, g) for pp in range(4) for g in range(3)]

            def proj(k):
                d = GROUPS[pairs[k][1]][1]
                wb = k % 2
                wqk_ring.need(k)
                for which in range(2):
                    for tq in range(4):
                        bk, br = next_bank()
                        for dc in range(8):
                            mm(bk[:, :], wqk[:, wb, which, dc, :], xT[:, dc, tq * 512:(tq + 1) * 512],
                               dc == 0, dc == 7, xT_r + [wqk_r[wb]], [br])
                        dst = qk[:, wb, which, :].rearrange("p (r l) -> p r l", r=d)[:, :, tq * 512 // d:(tq + 1) * 512 // d]
                        src = bk[:, :].rearrange("p (m r) -> p r m", r=d)
                        evac_copy(dst, src, [br], [qk_r[wb * 2 + which]], scale=(0.125 if which == 0 else None))
                        bfree(br)

            proj(0)
            for pair_i, (pp, g) in enumerate(pairs):
                if True:
                    win, d = GROUPS[g]
                    L = S // d
                    nkc = L // 128
                    wb = pair_i % 2
                    def make_head(e2, g=g, d=d, L=L, nkc=nkc, pp=pp, wb=wb):
                        hg = 2 * pp + e2
                        h = 8 * g + hg
                        b0 = 64 * e2
                        qT = qk[b0:b0 + 64, wb, 0, :]
                        kT = qk[b0:b0 + 64, wb, 1, :]
                        qkres = [qk_r[wb * 2], qk_r[wb * 2 + 1]]
                        visits = []
                        for c in range(16):
                            seg, kc = c // nkc, c % nkc
                            qa, qb = max(0, 128 * kc - 64), min(L, 128 * kc + 192)
                            j0 = qa - (128 * kc - 64)
                            if d == 16:
                                bank_i, bcol = c // 4, 128 * (c % 4)
                            else:
                                bank_i, bcol = c // 2, 256 * (c % 2) + j0
                            visits.append((c, seg * L + qa, qb - qa, j0, bank_i, bcol))
                        nb = visits[-1][4] + 1
                        sbanks = {}
                        otile = {}

                        def emit_qk(bi):
                            bk, br = next_bank()
                            sbanks[bi] = (bk, br)
                            first = True
                            thunks = []
                            for (c, q0, w, j0, bank_i, bcol) in visits:
                                if bank_i != bi:
                                    continue
                                thunks.append((lambda c, q0, w, bcol, first: lambda: mm(
                                    bk[:, bcol:bcol + w], kT[:, c * 128:(c + 1) * 128], qT[:, q0:q0 + w],
                                    first, False, qkres, [br], skip=True))(c, q0, w, bcol, first))
                                first = False
                            return thunks

                        def emit_exp(bi, pb):
                            bk, br = sbanks.pop(bi)
                            vs = [v for v in visits if v[4] == bi]
                            c0 = min(v[5] for v in vs)
                            c1 = max(v[5] + v[2] for v in vs)
                            kb.op("act", lambda e: e.activation(out=PT[:, pb, c0:c1], in_=bk[:, c0:c1], func=AF.Exp),
                                  [br], [PT_r[pb]])
                            bfree(br)
                            tcols = tab[:, h, c0:c1]
                            kb.op("dve", lambda e: e.tensor_tensor(out=PT[:, pb, c0:c1], in0=PT[:, pb, c0:c1],
                                                                   in1=tcols, op=ALU.mult),
                                  [PT_r[pb]] + tab_r, [PT_r[pb]])

                        def emit_pv(bi, pb, pend):
                            vs = [v for v in visits if v[4] == bi]
                            for (c, q0, w, j0, bank_i, bcol) in vs:
                                pos = q0
                                while pos < q0 + w:
                                    ot = pos // 512
                                    end = min(q0 + w, (ot + 1) * 512)
                                    if ot not in otile:
                                        otile[ot] = next_bank() + (True,)
                                    ob, obr, first = otile[ot]
                                    otile[ot] = (ob, obr, False)
                                    mm(ob[0:65, pos - ot * 512:end - ot * 512], V[:, g, c, hg, :],
                                       PT[:, pb, bcol + pos - q0:bcol + end - q0],
                                       first, False, [V_r[g], PT_r[pb]], [obr], skip=True)
                                    pos = end
                            nxt = [v for v in visits if v[4] > bi]
                            lim = nxt[0][1] if nxt else S
                            for ot in sorted(otile):
                                if (ot + 1) * 512 <= lim:
                                    ob, obr, _f = otile.pop(ot)

                                    def merge(ot=ot, ob=ob, obr=obr):
                                        if g == 0:
                                            av = acc[:, e2, ot * 512:(ot + 1) * 512]
                                            kb.op("dve", lambda e: e.tensor_copy(out=av, in_=ob[0:65, :]),
                                                  [obr], [acc_r[e2]])
                                        elif g == 1:
                                            av = acc[:, e2, ot:S:4]
                                            kb.op("dve", lambda e: e.tensor_tensor(out=av, in0=ob[0:65, :], in1=av, op=ALU.add),
                                                  [obr, acc_r[e2]], [acc_r[e2]])
                                        else:
                                            av = acc[:, e2, :].rearrange("p (n r) -> p r n", r=16)[:, 4 * ot:4 * ot + 4, :]
                                            src = ob[0:65, :].rearrange("p (j n) -> p j n", j=4)
                                            kb.op("dve", lambda e: e.tensor_tensor(out=av, in0=src, in1=av, op=ALU.add),
                                                  [obr, acc_r[e2]], [acc_r[e2]])
                                        bfree(obr)

                                    pend.append(merge)
                            if not nxt:
                                assert not otile

                        return nb, emit_qk, emit_exp, emit_pv

                    heads = [make_head(0), make_head(1)]
                    items = [(hd, bi) for bi in range(heads[0][0]) for hd in heads]
                    LA = 1

                    def qk_pair(j):
                        if 2 * j >= len(items):
                            return
                        ta = items[2 * j][0][1](items[2 * j][1])
                        tb = items[2 * j + 1][0][1](items[2 * j + 1][1])
                        for x in range(max(len(ta), len(tb))):
                            if x < len(ta):
                                ta[x]()
                            if x < len(tb):
                                tb[x]()

                    for j in range(LA):
                        qk_pair(j)
                    if pair_i + 1 < len(pairs):
                        proj(pair_i + 1)
                    norm_fns = []
                    if pending_norm is not None:
                        norm_fns = norm_mul(pending_norm)
                        pending_norm = None
                    pend = []
                    pbs = {}

                    def do_exp(ii):
                        pbs[ii] = pt_cnt[0] % 4
                        pt_cnt[0] += 1
                        items[ii][0][2](items[ii][1], pbs[ii])

                    do_exp(0)
                    for ii, it in enumerate(items):
                        if ii % 2 == 0:
                            qk_pair(ii // 2 + LA)
                        if ii + 1 < len(items):
                            do_exp(ii + 1)
                        for _ in range(2):
                            if norm_fns:
                                norm_fns.pop(0)()
                        for mfn in pend:
                            mfn()
                        pend = []
                        it[0][3](it[1], pbs[ii], pend)
                    for mfn in pend:
                        mfn()
                if g == 2:
                    norm_dma()
                    pending_norm = pp
            for fn_ in norm_mul(pending_norm):
                fn_()
            mem.release()
            if debug is not None and debug[0] == "attnT" and s == 0:
                finish_debug(attnT, attnT_r)

            reft, reft_r, _ = palloc("reft", BF16, [4, S], nres=4)
            wao, wao_r, _ = palloc("wao", BF16, [4, D])
            wfo, wfo_r, _ = palloc("wfo", BF16, [4, D])
            dma("pool", wao, wao_d, writes=wao_r)
            dma("pool", wfo, wfo_d, writes=wfo_r)
            mem.mark()
            UT, UT_r, _ = palloc("UT", BF16, [4, S], nres=16)
            UTe, UTe_r, _ = palloc("UTe", BF16, [4, S // 2])
            UTo, UTo_r, _ = palloc("UTo", BF16, [4, S // 2])
            AB, AB_r, _ = palloc("AB", BF16, [8, 4, 256], nres=1)
            a1024, a1024_r, _ = palloc("a1024", BF16, [4, 128], parts=1)
            c1024, c1024_r, _ = palloc("c1024", BF16, [S], parts=1)
            dftb, dftb_r, _ = palloc("dftb", BF16, [2, 2, 8, 512], nres=2)
            dma("pool", c1024, c1024_d, writes=c1024_r)
            dft_ring = Ring(4, 2, lambda k, b: dma("sp", dftb[:, b].rearrange("p a c n -> p (a c n)"), dft_s[k],
                                                   reads=[dft_sr[k]], writes=[dftb_r[b]]))
            dft_ring.need(0)
            for gq in range(4):
                for tq in range(4):
                    bk, br = next_bank()
                    for dc in range(8):
                        mm(bk[:, :], wf[:, dc, gq * 128:(gq + 1) * 128], xT[:, dc, tq * 512:(tq + 1) * 512],
                           dc == 0, dc == 7, xT_r + wf_r, [br])
                    evac_copy(UT[:, gq, tq * 512:(tq + 1) * 512], bk[:, :], [br], [UT_r[gq * 4 + tq]], eng="act")
                    bfree(br)
            H = S // 2
            kb.op("dve", lambda e: e.tensor_tensor(out=UTe[:, :, 1:H], in0=UT[:, :, 1:H], in1=UT[:, :, S - 1:H:-1],
                                                   op=ALU.add), UT_r, UTe_r)
            kb.op("dve", lambda e: e.tensor_copy(out=UTe[:, :, 0:1], in_=UT[:, :, 0:1]), UT_r, UTe_r)
            kb.op("dve", lambda e: e.tensor_tensor(out=UTo[:, :, 1:H], in0=UT[:, :, 1:H], in1=UT[:, :, S - 1:H:-1],
                                                   op=ALU.subtract), UT_r, UTo_r)
            kb.op("dve", lambda e: e.memset(UTo[:, :, 0:1], 0.0), (), UTo_r)
            bk, br = next_bank()
            for gq in range(4):
                mm(bk[0:1, gq * 128:(gq + 1) * 128], UT[:, gq, H:H + 1], ccsc[:, 0:128], True, True,
                   UT_r + ccsc_r, [br], skip=True)
            evac_copy(a1024, bk[0:1, :].rearrange("p (g n) -> p g n", g=4), [br], a1024_r, eng="act")
            bfree(br)
            for c in range(8):
                for gp in range(2):
                    bk, br = next_bank()
                    for j in range(2):
                        gq = gp * 2 + j
                        mm(bk[:, j * 256:j * 256 + 128], UTe[:, gq, c * 128:(c + 1) * 128], ccsc[:, 0:128], True, True,
                           UTe_r + ccsc_r, [br], skip=True)
                        mm(bk[:, j * 256 + 128:(j + 1) * 256], UTo[:, gq, c * 128:(c + 1) * 128], ccsc[:, 128:256],
                           True, True, UTo_r + ccsc_r, [br], skip=True)
                    evac_copy(AB[:, c, gp * 2:gp * 2 + 2, :], bk[:, :].rearrange("p (j n) -> p j n", j=2), [br], AB_r)
                    bfree(br)
            for kt in range(4):
                fb = [next_bank() for _ in range(4)]
                dft_ring.need(kt)
                b = kt % 2
                for cs_ in range(2):
                    for c in range(8):
                        for gq in range(4):
                            mm(fb[gq][0][:, :], AB[:, c, gq, cs_ * 128:(cs_ + 1) * 128], dftb[:, b, cs_, c, :],
                               cs_ == 0 and c == 0, False, AB_r + [dftb_r[b]], [fb[gq][1]])
                for gq in range(4):
                    mm(fb[gq][0][:, :], a1024[0:1, gq, :], c1024[0:1, kt * 512:(kt + 1) * 512], False, True,
                       a1024_r + c1024_r, [fb[gq][1]])
                for gq in range(4):
                    evac_copy(reft[:, gq, kt * 512:(kt + 1) * 512], fb[gq][0][:, :], [fb[gq][1]], [reft_r[gq]])
                    bfree(fb[gq][1])
            mem.release()
            if debug is not None and debug[0] == "reft" and s == 0:
                finish_debug(reft, reft_r)

            mergedT, mergedT_r, mergedT_off = palloc("mergedT", BF16, [8, S], nres=8, high=True)
            wout, wout_r, _ = palloc("wout", BF16, [8, D], high=True)
            wup, wup_r, _ = palloc("wup", BF16, [2, 8, 512], nres=2, high=True)
            dma("sp", wout, wout_s, reads=wout_sr, writes=wout_r)
            wup_ring = Ring(32, 2, lambda k, b: dma("sp", wup[:, b], wup_s[k % 8], reads=[wup_sr[k % 8]],
                                                    writes=[wup_r[b]]))
            wup_ring.need(0)
            mem.mark()
            wg, wg_r, _ = palloc("wg", BF16, [4, 8, 128], nres=4)
            gts, gts_r, _ = palloc("gts", BF16, [2, 2, 512], nres=4)
            t12, t12_r, _ = palloc("t12", F32, [2, 2, 512], nres=4)
            wg_ring = Ring(8, 2, lambda k, b: dma(
                "pool", wg[:, 2 * b:2 * b + 2],
                wg_d.rearrange("(a c) p k n -> c p a k n", a=2)[k], writes=[wg_r[2 * b], wg_r[2 * b + 1]]))
            for dc in range(8):
                wg_ring.need(dc)
                wslots = [2 * (dc % 2), 2 * (dc % 2) + 1]
                for tq in range(4):
                    cs = slice(tq * 512, (tq + 1) * 512)
                    gbuf = (dc * 4 + tq) % 2
                    for af in range(2):
                        bk, br = next_bank()
                        for kc in range(8):
                            mm(bk[:, :], wg[:, wslots[af], kc, :], xT[:, kc, cs], kc == 0, kc == 7,
                               xT_r + [wg_r[wslots[af]]], [br])
                        kb.op("act", (lambda bk, gbuf, af: lambda e: e.activation(
                            out=gts[:, gbuf, af, :], in_=bk[:, :], func=AF.Sigmoid))(bk, gbuf, af),
                            [br], [gts_r[gbuf * 2 + af]])
                        bfree(br)
                    bka, bra = next_bank()
                    for hp in range(4):
                        mm(bka[:, :], wao[:, hp, dc * 128:(dc + 1) * 128], attnT[:, hp, cs], hp == 0, hp == 3,
                           attnT_r + wao_r, [bra])
                    kb.op("dve", (lambda bka, gbuf: lambda e: e.tensor_tensor(
                        out=t12[:, gbuf, 0, :], in0=bka[:, :], in1=gts[:, gbuf, 0, :], op=ALU.mult))(bka, gbuf),
                        [bra, gts_r[gbuf * 2]], [t12_r[gbuf * 2]])
                    bfree(bra)
                    bkf, brf = next_bank()
                    for gq in range(4):
                        mm(bkf[:, :], wfo[:, gq, dc * 128:(dc + 1) * 128], reft[:, gq, cs], gq == 0, gq == 3,
                           reft_r + wfo_r, [brf])
                    kb.op("dve", (lambda bkf, gbuf: lambda e: e.tensor_tensor(
                        out=t12[:, gbuf, 1, :], in0=bkf[:, :], in1=gts[:, gbuf, 1, :], op=ALU.mult))(bkf, gbuf),
                        [brf, gts_r[gbuf * 2 + 1]], [t12_r[gbuf * 2 + 1]])
                    bfree(brf)
                    kb.op("dve", (lambda gbuf, dc, cs: lambda e: e.tensor_tensor(
                        out=mergedT[:, dc, cs], in0=t12[:, gbuf, 0, :], in1=t12[:, gbuf, 1, :], op=ALU.add))(gbuf, dc, cs),
                        [t12_r[gbuf * 2], t12_r[gbuf * 2 + 1]], [mergedT_r[dc]])
            mem.release()
            mem.release()
            if debug is not None and debug[0] == "mergedT" and s == 0:
                finish_debug(mergedT, mergedT_r)

            mem.mark()
            xr, xr_r, _ = palloc("xr", F32, [3, D], nres=3)
            z1, z1_r, _ = palloc("z1", F32, [2, D], nres=2)
            ah1T, ah1T_r, _ = palloc("ah1T", F32, [2, 8, 512], nres=16)
            h1T, h1T_r, _ = palloc("h1T", BF16, [8, 512], nres=8)
            hidT, hidT_r, _ = palloc("hidT", BF16, [32, 512], nres=32)
            rr, rr_r, _ = palloc("rr", F32, [2, 512], nres=2)
            wdn, wdn_r, _ = palloc("wdn", BF16, [2, 32, 128], nres=2)
            zn2, zn2_r, zn2_off = palloc("zn2", F32, [3, D], nres=3)
            st, st_r, _ = palloc("st", F32, [4, 16], nres=4)
            xr_ring = Ring(16, 3, lambda k, b: dma("sp", xr[:, b, :], x_d[s, k * 128:(k + 1) * 128, :],
                                                   writes=[xr_r[b]]))
            wdn_ring = Ring(32, 2, lambda k, b: dma("sp", wdn[:, b], wdn_s[k % 8], reads=[wdn_sr[k % 8]],
                                                    writes=[wdn_r[b]]))
            xr_ring.need(0)
            wdn_ring.need(0)
            cnt = {"z": 0, "st": 0, "zn": 0}
            stores = []

            def ln_stats(src0, src1, sb, reads):
                for half, src in enumerate((src0, src1)):
                    kb.op("dve", (lambda src, half: lambda e: e.bn_stats(
                        out=st[:, sb, half * 6:half * 6 + 6], in_=src))(src, half), reads[half], [st_r[sb]])
                kb.op("dve", lambda e: e.bn_aggr(
                    out=st[:, sb, 12:14], in_=st[:, sb, 0:12].rearrange("p (a b) -> p a b", a=2)),
                    [st_r[sb]], [st_r[sb]])
                kb.op("dve", lambda e: e.tensor_scalar(
                    out=st[:, sb, 13:14], in0=st[:, sb, 13:14], scalar1=EPS, scalar2=None, op0=ALU.add),
                    [st_r[sb]], [st_r[sb]])
                kb.op("pool", lambda e: e.tensor_tensor(
                    out=st[:, sb, 14:15], in0=st[:, sb, 13:14], in1=mhalf[:, 0:1], op=ALU.pow),
                    [st_r[sb]] + mhalf_r, [st_r[sb]])
                kb.op("dve", lambda e: e.scalar_tensor_tensor(
                    out=st[:, sb, 15:16], in0=st[:, sb, 12:13], scalar=-1.0, in1=st[:, sb, 14:15],
                    op0=ALU.mult, op1=ALU.mult), [st_r[sb]], [st_r[sb]])

            def ln1_mix(tq, sub):
                tt = tq * 4 + sub
                tok = slice(tt * 128, (tt + 1) * 128)
                xr_ring.need(tt)
                xb = tt % 3
                zb = cnt["z"] % 2
                cnt["z"] += 1
                sb = cnt["st"] % 4
                cnt["st"] += 1
                for half in range(2):
                    bk, br = next_bank()
                    for dc in range(8):
                        mm(bk[:, :], mergedT[:, dc, tok], wout[:, dc, half * 512:(half + 1) * 512], dc == 0, dc == 7,
                           mergedT_r + wout_r, [br])
                    kb.op("dve", (lambda bk, half: lambda e: e.scalar_tensor_tensor(
                        out=z1[:, zb, half * 512:(half + 1) * 512], in0=xr[:, xb, half * 512:(half + 1) * 512],
                        scalar=ALPHA, in1=bk[:, :], op0=ALU.mult, op1=ALU.add))(bk, half),
                        [br, xr_r[xb]], [z1_r[zb]])
                    bfree(br)
                ln_stats(z1[:, zb, 0:512], z1[:, zb, 512:1024], sb, [[z1_r[zb]], [z1_r[zb]]])
                kb.op("act", lambda e: e.activation(
                    out=z1[:, zb, :], in_=z1[:, zb, :], func=AF.Identity, scale=st[:, sb, 14:15],
                    bias=st[:, sb, 15:16]), [st_r[sb], z1_r[zb]], [z1_r[zb]])
                return zb

            def ln1_tr(tq, sub, zb):
                ab = tq % 2
                for half in range(2):
                    bk, br = next_bank()
                    for j in range(4):
                        dc = half * 4 + j
                        tr(bk[:, j * 128:(j + 1) * 128], z1[:, zb, dc * 128:(dc + 1) * 128], [z1_r[zb]], [br])
                    for j in range(4):
                        dc = half * 4 + j
                        for (dst, dst_r, so, bo) in ((h1T[:, dc, sub * 128:(sub + 1) * 128], h1T_r[dc], G1, B1),
                                                     (ah1T[:, ab, dc, sub * 128:(sub + 1) * 128], ah1T_r[ab * 8 + dc], AG1, AB1)):
                            if half == 0:
                                kb.op("act", (lambda bk, j, dc, dst, so, bo: lambda e: e.activation(
                                    out=dst, in_=bk[:, j * 128:(j + 1) * 128],
                                    func=AF.Identity, scale=pvec[:, so + dc:so + dc + 1],
                                    bias=pvec[:, bo + dc:bo + dc + 1]))(bk, j, dc, dst, so, bo),
                                    [br] + pvec_r, [dst_r])
                            else:
                                kb.op("dve", (lambda bk, j, dc, dst, so, bo: lambda e: e.tensor_scalar(
                                    out=dst, in0=bk[:, j * 128:(j + 1) * 128],
                                    scalar1=pvec[:, so + dc:so + dc + 1], scalar2=pvec[:, bo + dc:bo + dc + 1],
                                    op0=ALU.mult, op1=ALU.add))(bk, j, dc, dst, so, bo),
                                    [br] + pvec_r, [dst_r])
                    bfree(br)

            def ln2(tq):
                for sub in range(4):
                    ln2_sub(tq, sub)

            def ln2_sub(tq, sub):
                ab = tq % 2
                if True:
                    tt = tq * 4 + sub
                    tok = slice(tt * 128, (tt + 1) * 128)
                    zb = cnt["zn"] % 3
                    cnt["zn"] += 1
                    sb = cnt["st"] % 4
                    cnt["st"] += 1
                    hb = []
                    for half in range(2):
                        bk, br = next_bank()
                        hb.append((bk, br))
                        for j in range(4):
                            dc = half * 4 + j
                            tr(bk[:, j * 128:(j + 1) * 128], ah1T[:, ab, dc, sub * 128:(sub + 1) * 128],
                               [ah1T_r[ab * 8 + dc]], [br])
                    ln_stats(hb[0][0][:, :], hb[1][0][:, :], sb, [[hb[0][1]], [hb[1][1]]])
                    bk, br = hb[0]
                    kb.op("dve", (lambda bk: lambda e: e.scalar_tensor_tensor(
                        out=zn2[:, zb, 0:512], in0=bk[:, :], scalar=st[:, sb, 12:13], in1=g2b[:, 0:512],
                        op0=ALU.subtract, op1=ALU.mult))(bk), [br, st_r[sb]] + g2b_r, [zn2_r[zb]])
                    bfree(br)
                    kb.op("dve", lambda e: e.scalar_tensor_tensor(
                        out=zn2[:, zb, 0:512], in0=zn2[:, zb, 0:512], scalar=st[:, sb, 14:15], in1=b2b[:, 0:512],
                        op0=ALU.mult, op1=ALU.add), [zn2_r[zb], st_r[sb]] + b2b_r, [zn2_r[zb]])
                    bk, br = hb[1]
                    kb.op("act", (lambda bk: lambda e: e.activation(
                        out=zn2[:, zb, 512:1024], in_=bk[:, :], func=AF.Identity,
                        scale=st[:, sb, 14:15], bias=st[:, sb, 15:16]))(bk), [br, st_r[sb]], [zn2_r[zb]])
                    bfree(br)
                    kb.op("pool", lambda e: e.tensor_tensor(
                        out=zn2[:, zb, 512:1024], in0=zn2[:, zb, 512:1024], in1=g2b[:, 512:1024], op=ALU.mult),
                        [zn2_r[zb]] + g2b_r, [zn2_r[zb]])
                    kb.op("pool", lambda e: e.tensor_tensor(
                        out=zn2[:, zb, 512:1024], in0=zn2[:, zb, 512:1024], in1=b2b[:, 512:1024], op=ALU.add),
                        [zn2_r[zb]] + b2b_r, [zn2_r[zb]])
                    stores.append((lambda dst, srcap, rs: lambda: dma("sp", dst, srcap, reads=rs, is_out=True))(
                        out_d[s, tok, :], zn2[:, zb, :], [zn2_r[zb]]))

            def up_fg(tq, fg):
                k = tq * 8 + fg
                wup_ring.need(k)
                ub = k % 2
                for j in range(4):
                    fc = fg * 4 + j
                    bk, br = next_bank()
                    for dc in range(8):
                        mm(bk[:, :], wup[:, ub, dc, j * 128:(j + 1) * 128], h1T[:, dc, :], dc == 0, dc == 7,
                           h1T_r + [wup_r[ub]], [br])
                    rb = fc % 2
                    kb.op("dve", (lambda bk, rb, fc: lambda e: e.tensor_scalar(
                        out=rr[:, rb, :], in0=bk[:, :], scalar1=pvec[:, BUP + fc:BUP + fc + 1], scalar2=0.0,
                        op0=ALU.add, op1=ALU.max))(bk, rb, fc), [br] + pvec_r, [rr_r[rb]])
                    bfree(br)
                    kb.op("act", (lambda rb, fc: lambda e: e.activation(
                        out=hidT[:, fc, :], in_=rr[:, rb, :], func=AF.Square))(rb, fc), [rr_r[rb]], [hidT_r[fc]])

            def down_dc(tq, dc):
                ab = tq % 2
                k = tq * 8 + dc
                wdn_ring.need(k)
                db = k % 2
                bk, br = next_bank()
                for fc in range(32):
                    mm(bk[:, :], wdn[:, db, fc, :], hidT[:, fc, :], fc == 0, fc == 31, hidT_r + [wdn_r[db]], [br])
                kb.op("dve", (lambda bk, dc, ab: lambda e: e.scalar_tensor_tensor(
                    out=ah1T[:, ab, dc, :], in0=bk[:, :], scalar=pvec[:, BDN + dc:BDN + dc + 1], in1=ah1T[:, ab, dc, :],
                    op0=ALU.add, op1=ALU.add))(bk, dc, ab), [br, ah1T_r[ab * 8 + dc]] + pvec_r, [ah1T_r[ab * 8 + dc]])
                bfree(br)

            pend = None
            for sub in range(4):
                zb = ln1_mix(0, sub)
                if pend is not None:
                    ln1_tr(0, pend[0], pend[1])
                pend = (sub, zb)
            ln1_tr(0, pend[0], pend[1])
            if debug is not None and debug[0] == "h1T":
                finish_debug(h1T, h1T_r)
            for tq in range(4):
                for fg in range(8):
                    up_fg(tq, fg)
                    if tq > 0 and 2 <= fg < 6:
                        stores.pop(0)()
                    if tq > 0 and fg < 4:
                        ln2_sub(tq - 1, fg)
                zbs = {}
                nxt_x = None
                if tq == 3 and s + 1 < nseq:
                    nxs_r = mem.alloc_over("xs_n", mergedT_off, 3 * D * 4, 3)
                    nxs = mem.view(mergedT_off, F32, [3, D])
                    nxT_r = mem.alloc_over("xT_n", xT_off, 8 * S * 2, 16)
                    nxT = mem.view(xT_off, BF16, [8, S])
                    nring = Ring(16, 3, (lambda nxs, nxs_r: lambda k, b: dma(
                        "sp", nxs[:, b, :], x_d[s + 1, k * 128:(k + 1) * 128, :], writes=[nxs_r[b]]))(nxs, nxs_r))
                    nring.need(0)
                    nxt_x = (nxT, nxT_r, nxs, nxs_r, nring)
                    pre["xT"] = nxT_r
                for dc in range(8):
                    down_dc(tq, dc)
                    if nxt_x is not None:
                        for tt in (2 * dc, 2 * dc + 1):
                            phase0_tile(s + 1, tt, nxt_x[0], nxt_x[1], nxt_x[2], nxt_x[3], nxt_x[4])
                    if tq < 3:
                        if 1 <= dc <= 4:
                            ln1_tr(tq + 1, dc - 1, zbs[dc - 1])
                        if dc < 4:
                            zbs[dc] = ln1_mix(tq + 1, dc)
            for sub in range(4):
                ln2_sub(3, sub)
                if len(stores) > 1:
                    stores.pop(0)()
            if s + 1 < nseq:
                late_stores.extend([list(stores), zn2_off, zn2_off + 3 * D * 4])
                stores = []
            while stores:
                stores.pop(0)()
            mem.release()
            mem.release()
            for reg in mem.regions:
                if reg[0] >= mem.htop:
                    reg[3] = False
            mem.htop = POOL_BYTES
    except _Stop:
        pass

    kb.emit()
    return nc


def host_inputs(x, rel_bias, w_in, w_fourier_out, w_attn_out, w_out, ln1_g, ln1_b, w_up, b_up, w_down, b_down,
                ln2_g, ln2_b):
    f32 = np.float32
    w_in = np.asarray(w_in, f32)[0]
    Q, K_, V_, F_, GA, GF = 0, 1536, 3072, 4608, 5120, 6144

    def tile_w(w):
        return np.ascontiguousarray(w.reshape(8, 128, -1).transpose(1, 0, 2))

    wqk = np.stack([np.stack([tile_w(w_in[:, Q + 128 * p:Q + 128 * (p + 1)]),
                              tile_w(w_in[:, K_ + 128 * p:K_ + 128 * (p + 1)])]) for p in range(12)])
    wv = np.stack([tile_w(w_in[:, V_ + 512 * g:V_ + 512 * (g + 1)]) for g in range(3)])
    wf = tile_w(w_in[:, F_:F_ + 512])
    wg = np.stack([tile_w(w_in[:, GA + 128 * i:GA + 128 * (i + 1)]) for i in range(16)])
    wao = np.ascontiguousarray(np.asarray(w_attn_out, f32)[0].reshape(4, 128, D).transpose(1, 0, 2))
    wfo = np.ascontiguousarray(np.asarray(w_fourier_out, f32)[0].reshape(4, 128, D).transpose(1, 0, 2))
    wout = tile_w(np.asarray(w_out, f32)[0])
    wu = np.asarray(w_up, f32)[0]
    wup = np.stack([tile_w(wu[:, 512 * i:512 * (i + 1)]) for i in range(8)])
    wd = np.asarray(w_down, f32)[0]
    wdn = np.stack([np.ascontiguousarray(wd[:, 128 * dc:128 * (dc + 1)].reshape(32, 128, 128).transpose(1, 0, 2))
                    for dc in range(8)])
    rb = np.asarray(rel_bias, f32)
    i = np.arange(128)[:, None]
    j = np.arange(256)[None, :]
    rel = i - j + 64
    valid = np.abs(rel) <= 64
    tab = np.empty((128, 24, 256), f32)
    for g, (win, d) in enumerate(GROUPS):
        bk = t5_buckets(rel * d)
        for hh in range(8):
            h = 8 * g + hh
            tab[:, h, :] = np.where(valid, rb[bk, h], f32(NEG))
    n = np.arange(S, dtype=np.int64)
    ang = 2.0 * np.pi * ((n[:, None] * n[None, :]) % S).astype(np.float64) / S
    cs = np.stack([np.cos(ang), np.sin(ang)]) / np.sqrt(S)
    c1024 = np.ascontiguousarray(cs[0, S // 2:S // 2 + 1, :]).astype(f32)
    dft = cs[:, :S // 2].reshape(2, 8, 128, 4, 512).transpose(3, 2, 0, 1, 4)
    dft = np.ascontiguousarray(dft).astype(f32)
    m = np.arange(128, dtype=np.int64)
    a2 = 2.0 * np.pi * ((m[:, None] * m[None, :]) % 128).astype(np.float64) / 128
    ccsc = (np.concatenate([np.cos(a2), -np.sin(a2)], axis=1) / np.sqrt(128.0)).astype(f32)
    identf = np.eye(128, dtype=f32)
    identb = np.eye(128).astype(ml_dtypes.bfloat16)

    def pp(v):
        return np.asarray(v, f32).reshape(-1, 128).T

    g1, b1 = np.asarray(ln1_g, f32)[0], np.asarray(ln1_b, f32)[0]
    pvec_parts = [pp(g1), pp(b1), None, None, pp(np.asarray(b_down, f32)[0]), pp(np.asarray(b_up, f32)[0])]
    shared = dict(wqk=wqk, wv=wv, wf=wf, wg=wg, wao=wao, wfo=wfo, wout=wout, wup=wup, wdn=wdn, tab=tab, dft=dft,
                  ccsc=ccsc, c1024=c1024, identf=identf,
                  dft_s=np.zeros((4, 128, 2 * 8 * 512), ml_dtypes.bfloat16), identb=identb,
                  g2b=np.ascontiguousarray(np.broadcast_to(np.asarray(ln2_g, f32)[0], (128, D))),
                  b2b=np.ascontiguousarray(np.broadcast_to(np.asarray(ln2_b, f32)[0], (128, D))))
    return shared, pvec_parts


_CACHE = {}


def kernel(x, rel_bias, w_in, w_fourier_out, w_attn_out, w_out, ln1_g, ln1_b, w_up, b_up, w_down, b_down,
           ln2_g, ln2_b, _debug=None, _ncores=NCORES):
    x = np.asarray(x, np.float32)
    shared, pv = host_inputs(x, rel_bias, w_in, w_fourier_out, w_attn_out, w_out, ln1_g, ln1_b, w_up, b_up,
                             w_down, b_down, ln2_g, ln2_b)
    pvec = np.zeros((128, 72), np.float32)
    pvec[:, 0:8] = pv[0]
    pvec[:, 8:16] = pv[1]
    pvec[:, 32:40] = pv[4]
    pvec[:, 40:72] = pv[5]
    shared["pvec"] = pvec
    nc = build(debug=_debug)
    in_maps = []
    for c in range(_ncores):
        m = dict(shared)
        m["x"] = np.ascontiguousarray(x[2 * c:2 * c + 2])
        in_maps.append(m)
    res = run_bass_kernel_spmd(nc, in_maps, core_ids=list(range(_ncores)))
    if _debug is not None:
        return res.results
    return np.concatenate([r["out"] for r in res.results], axis=0)
```

```python
import numpy as np
import ml_dtypes
import concourse.bass as bass
import concourse.mybir as mybir
from concourse.bass_utils import run_bass_kernel_spmd

F32 = mybir.dt.float32
BF16 = mybir.dt.bfloat16
AF = mybir.ActivationFunctionType
ALU = mybir.AluOpType

NCORES = 8
S = 2048
D = 1024
NSEQ = 2
DFF = 4096
ALPHA = 2.0 ** 0.25
EPS = 1e-5
NEG = -30000.0
GROUPS = ((128, 1), (512, 4), (2048, 16))
POOL_BYTES = 206 * 1024


class Res:
    __slots__ = ("name", "writers", "readers", "inherit", "excl")

    def __init__(self, name, inherit=(), excl=False):
        self.name = name
        self.excl = excl
        self.writers = []
        self.readers = []
        self.inherit = list(inherit)


class Op:
    __slots__ = ("eng", "fn", "deps", "dma", "sem", "val", "signal", "idx")


class KB:
    ENGS = ("pe", "act", "dve", "pool", "sp")

    def __init__(self, nc):
        self.nc = nc
        self.ops = {e: [] for e in self.ENGS}
        self.n = 0
        self.ndma = 0
        self.NDMASEM = 12
        self.nsw = 0
        self.dma_last = [None] * self.NDMASEM
        self.dma_cnt = [0] * self.NDMASEM
        self.out_dmas = []

    def op(self, eng, fn, reads=(), writes=(), dma=False, is_out=False):
        o = Op()
        o.eng, o.fn, o.dma, o.signal, o.idx = eng, fn, dma, False, self.n
        self.n += 1
        deps = []
        for r in reads:
            deps += r.inherit
            deps += r.writers
            if r.excl:
                deps += [q for q in r.readers if q.eng != eng]
        for r in writes:
            deps += r.inherit
            deps += r.writers
            deps += r.readers
        if dma and eng == "pool":
            o.sem, o.val = ("sw", self.nsw), 16
            self.nsw += 1
            o.signal = True
        elif dma:
            j = self.ndma % self.NDMASEM
            self.ndma += 1
            if self.dma_last[j] is not None:
                deps.append(self.dma_last[j])
            self.dma_cnt[j] += 1
            o.sem, o.val = j, 16 * self.dma_cnt[j]
            self.dma_last[j] = o
            o.signal = True
        else:
            o.sem, o.val = None, None
        seen = set()
        o.deps = []
        for d in deps:
            if d.idx in seen or d is o:
                continue
            seen.add(d.idx)
            if d.eng == "pe" and eng == "pe" and not d.dma:
                continue
            o.deps.append(d)
            d.signal = True
        for r in reads:
            r.readers.append(o)
        for r in writes:
            r.writers = [o]
            r.readers = []
            r.inherit = []
        self.ops[eng].append(o)
        if is_out:
            self.out_dmas.append(o)
        return o

    def emit(self):
        nc = self.nc
        esem = {e: nc.alloc_semaphore("s_" + e) for e in self.ENGS}
        dsem = {j: nc.alloc_semaphore("d_%d" % j) for j in range(self.NDMASEM)}
        for j in range(self.nsw):
            dsem[("sw", j)] = nc.alloc_semaphore("w_%d" % j)
        for e in self.ENGS:
            t = 0
            for o in self.ops[e]:
                if o.dma:
                    continue
                if o.signal:
                    t += 1
                    o.val = t
        final_waits = [(dsem[o.sem], o.val) for o in self.out_dmas]

        def run(e, eng):
            waited = {}
            for o in self.ops[e]:
                for d in o.deps:
                    if d.dma:
                        key, sem, val = ("d", d.sem), dsem[d.sem], d.val
                    else:
                        key, sem, val = ("e", d.eng), esem[d.eng], d.val
                    if waited.get(key, 0) >= val:
                        continue
                    waited[key] = val
                    eng.wait_ge(sem, val)
                ins = o.fn(eng)
                if o.dma:
                    ins.then_inc(dsem[o.sem], 16)
                elif o.signal:
                    ins.then_inc(esem[e], 1)
            if e == "sp":
                for sem, val in final_waits:
                    eng.wait_ge(sem, val)

        with nc.Block() as block:
            @block.sync
            def _(eng):
                run("sp", eng)

            @block.scalar
            def _(eng):
                run("act", eng)

            @block.vector
            def _(eng):
                run("dve", eng)

            @block.gpsimd
            def _(eng):
                run("pool", eng)

            @block.tensor
            def _(eng):
                run("pe", eng)


class Mem:
    def __init__(self, nc):
        self.pool = nc.alloc_sbuf_tensor("pool", [128, POOL_BYTES // 4], F32)
        self.regions = []
        self.top = 0
        self.htop = POOL_BYTES
        self.marks = []

    def mark(self):
        self.marks.append((self.top, len(self.regions)))

    def release(self):
        top, n = self.marks.pop()
        for i in range(n, len(self.regions)):
            self.regions[i][3] = False
        self.top = top

    def alloc_over(self, name, start, nbytes, nres=1):
        inherit = []
        for (s, e, rs, alive) in self.regions:
            if s < start + nbytes and start < e:
                for r in rs:
                    inherit += r.writers + r.readers + r.inherit
        rs = [Res("%s_%d" % (name, i), inherit) for i in range(nres)]
        self.regions.append([start, start + nbytes, rs, True])
        return rs

    def alloc(self, name, nbytes, nres=1, high=False, preset=None):
        nbytes = (nbytes + 31) // 32 * 32
        if high:
            self.htop -= nbytes
            start = self.htop
        else:
            start = self.top
            self.top += nbytes
        assert self.top <= self.htop, (name, self.top, self.htop, nbytes)
        inherit = []
        for (s, e, rs, alive) in self.regions:
            if (not alive) and s < start + nbytes and start < e:
                for r in rs:
                    inherit += r.writers + r.readers + r.inherit
        rs = preset if preset is not None else [Res("%s_%d" % (name, i), inherit) for i in range(nres)]
        self.regions.append([start, start + nbytes, rs, True])
        return start, rs

    def add_late(self, ops, start, end):
        for (s, e, rs, alive) in self.regions:
            if alive and s < end and start < e:
                for r in rs:
                    r.inherit += ops

    def view(self, off, dtype, shape, parts=128, p0=0):
        n = 1
        for s_ in shape:
            n *= s_
        assert off % 4 == 0
        if dtype == F32:
            ap = self.pool[p0:p0 + parts, off // 4: off // 4 + n]
        else:
            assert n % 2 == 0
            ap = self.pool[p0:p0 + parts, off // 4: off // 4 + n // 2].bitcast(BF16)
        if len(shape) == 1:
            return ap
        names = ["a%d" % i for i in range(len(shape))]
        pat = "p (%s) -> p %s" % (" ".join(names), " ".join(names))
        kw = {names[i]: shape[i] for i in range(1, len(shape))}
        return ap.rearrange(pat, **kw)


def t5_buckets(rel):
    half = 16
    ret = np.where(rel > 0, half, 0)
    n = np.abs(rel)
    max_exact = half // 2
    n_f = np.maximum(n, 1).astype(np.float32)
    large = max_exact + (np.log(n_f / max_exact) / np.log(1024 / max_exact) * (half - max_exact)).astype(np.int32)
    large = np.minimum(large, half - 1)
    return (ret + np.where(n < max_exact, n, large)).astype(np.int32)


def build(debug=None, nseq=NSEQ):
    nc = bass.Bass("TRN2", target_bir_lowering=False)
    kb = KB(nc)
    mem = Mem(nc)

    def din(name, shape, dt=F32):
        return nc.dram_tensor(name, list(shape), dt, kind="ExternalInput").ap()

    x_d = din("x", [NSEQ, S, D])
    wqk_d = din("wqk", [12, 2, 128, 8, 128])
    wv_d = din("wv", [3, 128, 8, 512])
    wf_d = din("wf", [128, 8, 512])
    wg_d = din("wg", [16, 128, 8, 128])
    wao_d = din("wao", [128, 4, D])
    wfo_d = din("wfo", [128, 4, D])
    wout_d = din("wout", [128, 8, D])
    wup_d = din("wup", [8, 128, 8, 512])
    wdn_d = din("wdn", [8, 128, 32, 128])
    tab_d = din("tab", [128, 24, 256])
    dft_d = din("dft", [4, 128, 2, 8, 512])
    c1024_d = din("c1024", [1, S])
    ccsc_d = din("ccsc", [128, 256])
    identf_d = din("identf", [128, 128])
    identb_d = din("identb", [128, 128], BF16)
    pvec_d = din("pvec", [128, 72])
    g2b_d = din("g2b", [128, D])
    b2b_d = din("b2b", [128, D])
    out_d = nc.dram_tensor("out", [NSEQ, S, D], F32, kind="ExternalOutput").ap()
    dbg_d = None
    if debug is not None:
        dbg_d = nc.dram_tensor("dbg", list(debug[1]), debug[2], kind="ExternalOutput").ap()

    banks = [nc.alloc_psum_tensor("bank%d" % i, [128, 512], F32) for i in range(8)]
    bres = [Res("bank%d" % i, excl=True) for i in range(8)]
    bank_free = list(range(8))

    def next_bank():
        i = bank_free.pop(0)
        return banks[i], bres[i]

    def bfree(br):
        i = bres.index(br)
        assert i not in bank_free
        bank_free.append(i)

    def palloc(name, dtype, shape, parts=128, nres=1, high=False, preset=None):
        n = 1
        for s_ in shape:
            n *= s_
        off, rs = mem.alloc(name, n * (4 if dtype == F32 else 2), nres, high, preset)
        return mem.view(off, dtype, shape, parts), rs, off

    identf, identf_r, _ = palloc("identf", F32, [128])
    identb, identb_r, _ = palloc("identb", BF16, [128])
    onesf, onesf_r, _ = palloc("onesf", F32, [64])
    ccsc, ccsc_r, _ = palloc("ccsc", BF16, [256])
    pvec, pvec_r, _ = palloc("pvec", F32, [72])
    g2b, g2b_r, _ = palloc("g2b", F32, [D])
    b2b, b2b_r, _ = palloc("b2b", F32, [D])
    mhalf, mhalf_r, _ = palloc("mhalf", F32, [8])

    def dma(eng, out, in_, reads=(), writes=(), is_out=False):
        return kb.op(eng, lambda e: e.dma_start(out=out, in_=in_), reads, writes, dma=True, is_out=is_out)

    dma("sp", identf, identf_d, writes=identf_r)
    dma("sp", identb, identb_d, writes=identb_r)
    dma("pool", ccsc, ccsc_d, writes=ccsc_r)
    dma("sp", pvec, pvec_d, writes=pvec_r)
    dma("sp", g2b, g2b_d, writes=g2b_r)
    dma("sp", b2b, b2b_d, writes=b2b_r)
    kb.op("dve", lambda e: e.memset(onesf, 1.0), writes=onesf_r)
    kb.op("dve", lambda e: e.memset(mhalf, -0.5), writes=mhalf_r)
    G1, B1, AG1, AB1, BDN, BUP = 0, 8, 16, 24, 32, 40
    kb.op("dve", lambda e: e.tensor_scalar(out=pvec[:, 16:32], in0=pvec[:, 0:16], scalar1=ALPHA, scalar2=None,
                                           op0=ALU.mult), pvec_r, pvec_r)

    evac_rr = [0]

    def evac_copy(out, in_, reads, writes, scale=None, eng=None):
        if eng is None:
            eng = ("act", "dve")[evac_rr[0] % 2]
            evac_rr[0] += 1
        if eng == "act":
            if scale is None:
                return kb.op("act", lambda e: e.activation(out=out, in_=in_, func=AF.Copy), reads, writes)
            return kb.op("act", lambda e: e.activation(out=out, in_=in_, func=AF.Copy, scale=scale), reads, writes)
        if scale is None:
            return kb.op("dve", lambda e: e.tensor_copy(out=out, in_=in_), reads, writes)
        return kb.op("dve", lambda e: e.tensor_scalar(out=out, in0=in_, scalar1=scale, scalar2=None, op0=ALU.mult),
                     reads, writes)

    def mm(out, lhsT, rhs, start, stop, reads, writes, skip=False):
        if skip:
            return kb.op("pe", lambda e: e.matmul(out, lhsT, rhs, start=start, stop=stop, skip_group_check=True),
                         reads, writes)
        return kb.op("pe", lambda e: e.matmul(out, lhsT, rhs, start=start, stop=stop), reads, writes)

    def tr(out, in_, reads, writes):
        return kb.op("pe", lambda e: e.transpose(out, in_, identf), list(reads) + identf_r, writes)


    class Ring:
        def __init__(self, n, nbuf, load):
            self.n, self.nbuf, self.load, self.nxt = n, nbuf, load, 0

        def need(self, k):
            lim = min(self.n - 1, k + self.nbuf - 1)
            while self.nxt <= lim:
                self.load(self.nxt, self.nxt % self.nbuf)
                self.nxt += 1

    wout_s = nc.dram_tensor("wout_s", [128, 8 * D], BF16).ap()
    wup_s = nc.dram_tensor("wup_s", [8, 128, 8 * 512], BF16).ap()
    wdn_s = nc.dram_tensor("wdn_s", [8, 128, 32 * 128], BF16).ap()
    wout_sr = [Res("wout_s")]
    wup_sr = [Res("wup_s%d" % i) for i in range(8)]
    wdn_sr = [Res("wdn_s%d" % i) for i in range(8)]

    cast_jobs = [(wout_s, wout_d.rearrange("p c n -> p (c n)"), wout_sr)]
    for i in range(8):
        cast_jobs.append((wup_s[i], wup_d[i].rearrange("p c n -> p (c n)"), [wup_sr[i]]))
    for i in range(8):
        cast_jobs.append((wdn_s[i], wdn_d[i].rearrange("p c n -> p (c n)"), [wdn_sr[i]]))
    dft_s = din("dft_s", [4, 128, 2 * 8 * 512], BF16)
    dft_sr = [Res("dft_s%d" % i) for i in range(4)]
    for i in range(4):
        cast_jobs.insert(1 + 2 * i, (dft_s[i], dft_d[i].rearrange("p a c n -> p (a c n)"), [dft_sr[i]]))

    def emit_scratch_casts(n):
        for _ in range(n):
            if cast_jobs:
                o_, i_, r_ = cast_jobs.pop(0)
                dma("pool", o_, i_, writes=r_)

    den_s = nc.dram_tensor("den_s", [1, 2 * S], F32).ap()
    rden_s = nc.dram_tensor("rden_s", [1, 2 * S], F32).ap()
    den_sr = [Res("den_s")]
    rden_sr = [Res("rden_s")]

    class _Stop(Exception):
        pass

    def finish_debug(ap, rs):
        dma("sp", dbg_d, ap, reads=rs, is_out=True)
        raise _Stop()

    late_stores = []
    pre = {}

    def phase0_tile(sq, tt, xT_ap, xT_rs, xs_ap, xs_rs, ring):
        ring.need(tt)
        b = tt % 3
        for half in range(2):
            bk, br = next_bank()
            for j in range(4):
                dc = half * 4 + j
                tr(bk[:, j * 128:(j + 1) * 128], xs_ap[:, b, dc * 128:(dc + 1) * 128], [xs_rs[b]], [br])
            evac_copy(xT_ap[:, half * 4:half * 4 + 4, tt * 128:(tt + 1) * 128],
                      bk[:, :].rearrange("p (j t) -> p j t", j=4), [br], [xT_rs[tt]])
            bfree(br)

    try:
        for s in range(nseq):
            mem.mark()
            mem.mark()
            pre_xT = pre.pop("xT", None)
            xT, xT_r, xT_off = palloc("xT", BF16, [8, S], nres=16, preset=pre_xT)
            attnT, attnT_r, _ = palloc("attnT", BF16, [4, S], nres=4)
            wf, wf_r, _ = palloc("wf", BF16, [8, 512], nres=1)

            if pre_xT is None:
                xs, xs_r, _ = palloc("xs", F32, [3, D], nres=3, high=True)
                xs_ring = Ring(16, 3, lambda k, b: dma("sp", xs[:, b, :], x_d[s, k * 128:(k + 1) * 128, :],
                                                       writes=[xs_r[b]]))
                xs_ring.need(0)
            if late_stores:
                n0 = kb.n
                for fn_ in late_stores[0]:
                    fn_()
                late_ops = [o for o in kb.ops["sp"] if o.idx >= n0]
                mem.add_late(late_ops, late_stores[1], late_stores[2])
                late_stores.clear()
            mem.mark()
            tab, tab_r, _ = palloc("tab", BF16, [24, 512])
            V, V_r, _ = palloc("V", BF16, [3, 16, 8, 65], nres=3)
            Vm = V.rearrange("p g c h e -> p (g c h) e")[:, :, 64:65]
            kb.op("dve", (lambda Vm: lambda e: e.memset(Vm, 1.0))(Vm), writes=V_r)
            mem.mark()
            pre_wv = pre.pop("wv", None)
            wv, wv_r, wv_off = palloc("wv", BF16, [2, 8, 512], nres=2, preset=pre_wv)
            wv_ring = Ring(3, 2, lambda k, b: dma("pool", wv[:, b], wv_d[k], writes=[wv_r[b]]))
            if pre_wv is not None:
                wv_ring.nxt = 2
            wv_ring.need(0)
            tabraw, tabraw_r, _ = palloc("tabraw", BF16, [24, 256])

            def v_chunk(g, d, nkc, c):
                seg, kc = c // nkc, c % nkc
                t0 = (128 * kc) * d + seg
                bk, br = next_bank()
                for dc in range(8):
                    if d == 1:
                        lhsT = xT[:, dc, t0:t0 + 128]
                    else:
                        lhsT = xT[:, dc, t0:t0 + 127 * d + 1:d]
                    mm(bk[:, :], lhsT, wv[:, g % 2, dc, :], dc == 0, dc == 7, xT_r + [wv_r[g % 2]], [br])
                evac_copy(V[:, g, c, :, 0:64], bk[:, :].rearrange("p (h e) -> p h e", h=8), [br], [V_r[g]])
                bfree(br)

            for tt in range(16):
                if pre_xT is None:
                    phase0_tile(s, tt, xT, xT_r, xs, xs_r, xs_ring)
                if tt >= 1:
                    v_chunk(0, 1, 16, tt - 1)
            v_chunk(0, 1, 16, 15)
            if debug is not None and debug[0] == "xT" and s == 0:
                finish_debug(xT, xT_r)
            dma("pool", tabraw, tab_d, writes=tabraw_r)
            for i in range(2):
                kb.op("act", (lambda i: lambda e: e.activation(out=tab[:, 0:16, 256 * i:256 * i + 256],
                                                                in_=tabraw[:, 0:16, :], func=AF.Exp))(i),
                      tabraw_r, tab_r)
            for i in range(4):
                kb.op("act", (lambda i: lambda e: e.activation(out=tab[:, 16:24, 128 * i:128 * i + 128],
                                                                in_=tabraw[:, 16:24, 64:192], func=AF.Exp))(i),
                      tabraw_r, tab_r)
            for g, (win, d) in enumerate(GROUPS):
                if g == 0:
                    continue
                L = S // d
                nkc = L // 128
                wv_ring.need(g)
                for c in range(16):
                    v_chunk(g, d, nkc, c)
            for reg in mem.regions:
                if reg[0] >= mem.htop:
                    reg[3] = False
            mem.htop = POOL_BYTES
            mem.release()
            dma("pool", wf, wf_d, writes=wf_r)
            acc, acc_r, _ = palloc("acc", F32, [2, S], parts=65, nres=2)
            qk, qk_r, _ = palloc("qk", BF16, [2, 2, S], nres=4)
            PT, PT_r, _ = palloc("PT", BF16, [4, 512], nres=4)
            wqk, wqk_r, _ = palloc("wqk", BF16, [2, 2, 8, 128], nres=2)
            pair_order = [4 * g + pp for pp in range(4) for g in range(3)]
            def load_wqk(k, b):
                dma("pool", wqk[:, b], wqk_d[pair_order[k]].rearrange("a p c n -> p a c n"), writes=[wqk_r[b]])
                if k >= 1:
                    emit_scratch_casts(2)

            wqk_ring = Ring(12, 2, load_wqk)
            rd, rd_r, _ = palloc("rd", F32, [32])
            rbc, rbc_r, _ = palloc("rbc", F32, [2, S], parts=64)

            def norm_dma():
                dma("sp", den_s, acc[64:65, :, :].rearrange("o e t -> o (e t)"), reads=acc_r, writes=den_sr)
                dma("sp", rd, den_s.rearrange("o (p j) -> (o p) j", j=32), reads=den_sr, writes=rd_r)
                kb.op("dve", lambda e: e.reciprocal(out=rd, in_=rd), rd_r, rd_r)
                dma("sp", rden_s.rearrange("o (p j) -> (o p) j", j=32), rd, reads=rd_r, writes=rden_sr)
                dma("sp", rbc.rearrange("p e t -> p (e t)"), rden_s.to_broadcast([64, 2 * S]), reads=rden_sr, writes=rbc_r)

            def norm_mul(pp_):
                fns = []
                for e2 in range(2):
                    for tq in range(4):
                        cs = slice(tq * 512, (tq + 1) * 512)
                        fns.append((lambda cs, e2: lambda: kb.op("dve", lambda e: e.tensor_tensor(
                            out=attnT[64 * e2:64 * e2 + 64, pp_, cs], in0=acc[0:64, e2, cs], in1=rbc[:, e2, cs],
                            op=ALU.mult), [acc_r[e2]] + rbc_r, [attnT_r[pp_]]))(cs, e2))
                return fns

            pt_cnt = [0]
            pending_norm = None
            pairs = [(pp, g) for pp in range(4) for g in range(3)]

            def proj(k):
                d = GROUPS[pairs[k][1]][1]
                wb = k % 2
                wqk_ring.need(k)
                for which in range(2):
                    for tq in range(4):
                        bk, br = next_bank()
                        for dc in range(8):
                            mm(bk[:, :], wqk[:, wb, which, dc, :], xT[:, dc, tq * 512:(tq + 1) * 512],
                               dc == 0, dc == 7, xT_r + [wqk_r[wb]], [br])
                        dst = qk[:, wb, which, :].rearrange("p (r l) -> p r l", r=d)[:, :, tq * 512 // d:(tq + 1) * 512 // d]
                        src = bk[:, :].rearrange("p (m r) -> p r m", r=d)
                        evac_copy(dst, src, [br], [qk_r[wb * 2 + which]], scale=(0.125 if which == 0 else None))
                        bfree(br)

            proj(0)
            for pair_i, (pp, g) in enumerate(pairs):
                if True:
                    win, d = GROUPS[g]
                    L = S // d
                    nkc = L // 128
                    wb = pair_i % 2
                    def make_head(e2, g=g, d=d, L=L, nkc=nkc, pp=pp, wb=wb):
                        hg = 2 * pp + e2
                        h = 8 * g + hg
                        b0 = 64 * e2
                        qT = qk[b0:b0 + 64, wb, 0, :]
                        kT = qk[b0:b0 + 64, wb, 1, :]
                        qkres = [qk_r[wb * 2], qk_r[wb * 2 + 1]]
                        visits = []
                        for c in range(16):
                            seg, kc = c // nkc, c % nkc
                            qa, qb = max(0, 128 * kc - 64), min(L, 128 * kc + 192)
                            j0 = qa - (128 * kc - 64)
                            if d == 16:
                                bank_i, bcol = c // 4, 128 * (c % 4)
                            else:
                                bank_i, bcol = c // 2, 256 * (c % 2) + j0
                            visits.append((c, seg * L + qa, qb - qa, j0, bank_i, bcol))
                        nb = visits[-1][4] + 1
                        sbanks = {}
                        otile = {}

                        def emit_qk(bi):
                            bk, br = next_bank()
                            sbanks[bi] = (bk, br)
                            first = True
                            thunks = []
                            for (c, q0, w, j0, bank_i, bcol) in visits:
                                if bank_i != bi:
                                    continue
                                thunks.append((lambda c, q0, w, bcol, first: lambda: mm(
                                    bk[:, bcol:bcol + w], kT[:, c * 128:(c + 1) * 128], qT[:, q0:q0 + w],
                                    first, False, qkres, [br], skip=True))(c, q0, w, bcol, first))
                                first = False
                            return thunks

                        def emit_exp(bi, pb):
                            bk, br = sbanks.pop(bi)
                            vs = [v for v in visits if v[4] == bi]
                            c0 = min(v[5] for v in vs)
                            c1 = max(v[5] + v[2] for v in vs)
                            kb.op("act", lambda e: e.activation(out=PT[:, pb, c0:c1], in_=bk[:, c0:c1], func=AF.Exp),
                                  [br], [PT_r[pb]])
                            bfree(br)
                            tcols = tab[:, h, c0:c1]
                            kb.op("dve", lambda e: e.tensor_tensor(out=PT[:, pb, c0:c1], in0=PT[:, pb, c0:c1],
                                                                   in1=tcols, op=ALU.mult),
                                  [PT_r[pb]] + tab_r, [PT_r[pb]])

                        def emit_pv(bi, pb, pend):
                            vs = [v for v in visits if v[4] == bi]
                            for (c, q0, w, j0, bank_i, bcol) in vs:
                                pos = q0
                                while pos < q0 + w:
                                    ot = pos // 512
                                    end = min(q0 + w, (ot + 1) * 512)
                                    if ot not in otile:
                                        otile[ot] = next_bank() + (True,)
                                    ob, obr, first = otile[ot]
                                    otile[ot] = (ob, obr, False)
                                    mm(ob[0:65, pos - ot * 512:end - ot * 512], V[:, g, c, hg, :],
                                       PT[:, pb, bcol + pos - q0:bcol + end - q0],
                                       first, False, [V_r[g], PT_r[pb]], [obr], skip=True)
                                    pos = end
                            nxt = [v for v in visits if v[4] > bi]
                            lim = nxt[0][1] if nxt else S
                            for ot in sorted(otile):
                                if (ot + 1) * 512 <= lim:
                                    ob, obr, _f = otile.pop(ot)

                                    def merge(ot=ot, ob=ob, obr=obr):
                                        if g == 0:
                                            av = acc[:, e2, ot * 512:(ot + 1) * 512]
                                            kb.op("dve", lambda e: e.tensor_copy(out=av, in_=ob[0:65, :]),
                                                  [obr], [acc_r[e2]])
                                        elif g == 1:
                                            av = acc[:, e2, ot:S:4]
                                            kb.op("dve", lambda e: e.tensor_tensor(out=av, in0=ob[0:65, :], in1=av, op=ALU.add),
                                                  [obr, acc_r[e2]], [acc_r[e2]])
                                        else:
                                            av = acc[:, e2, :].rearrange("p (n r) -> p r n", r=16)[:, 4 * ot:4 * ot + 4, :]
                                            src = ob[0:65, :].rearrange("p (j n) -> p j n", j=4)
                                            kb.op("dve", lambda e: e.tensor_tensor(out=av, in0=src, in1=av, op=ALU.add),
                                                  [obr, acc_r[e2]], [acc_r[e2]])
                                        bfree(obr)

                                    pend.append(merge)
                            if not nxt:
                                assert not otile

                        return nb, emit_qk, emit_exp, emit_pv

                    heads = [make_head(0), make_head(1)]
                    items = [(hd, bi) for bi in range(heads[0][0]) for hd in heads]
                    LA = 1

                    def qk_pair(j):
                        if 2 * j >= len(items):
                            return
                        ta = items[2 * j][0][1](items[2 * j][1])
                        tb = items[2 * j + 1][0][1](items[2 * j + 1][1])
                        for x in range(max(len(ta), len(tb))):
                            if x < len(ta):
                                ta[x]()
                            if x < len(tb):
                                tb[x]()

                    for j in range(LA):
                        qk_pair(j)
                    if pair_i + 1 < len(pairs):
                        proj(pair_i + 1)
                    norm_fns = []
                    if pending_norm is not None:
                        norm_fns = norm_mul(pending_norm)
                        pending_norm = None
                    pend = []
                    pbs = {}

                    def do_exp(ii):
                        pbs[ii] = pt_cnt[0] % 4
                        pt_cnt[0] += 1
                        items[ii][0][2](items[ii][1], pbs[ii])

                    do_exp(0)
                    for ii, it in enumerate(items):
                        if ii % 2 == 0:
                            qk_pair(ii // 2 + LA)
                        if ii + 1 < len(items):
                            do_exp(ii + 1)
                        for _ in range(2):
                            if norm_fns:
                                norm_fns.pop(0)()
                        for mfn in pend:
                            mfn()
                        pend = []
                        it[0][3](it[1], pbs[ii], pend)
                    for mfn in pend:
                        mfn()
                if g == 2:
                    norm_dma()
                    pending_norm = pp
            for fn_ in norm_mul(pending_norm):
                fn_()
            mem.release()
            if debug is not None and debug[0] == "attnT" and s == 0:
                finish_debug(attnT, attnT_r)

            reft, reft_r, _ = palloc("reft", BF16, [4, S], nres=4)
            wao, wao_r, _ = palloc("wao", BF16, [4, D])
            wfo, wfo_r, _ = palloc("wfo", BF16, [4, D])
            dma("pool", wao, wao_d, writes=wao_r)
            dma("pool", wfo, wfo_d, writes=wfo_r)
            mem.mark()
            UT, UT_r, _ = palloc("UT", BF16, [4, S], nres=16)
            UTe, UTe_r, _ = palloc("UTe", BF16, [4, S // 2])
            UTo, UTo_r, _ = palloc("UTo", BF16, [4, S // 2])
            AB, AB_r, _ = palloc("AB", BF16, [8, 4, 256], nres=1)
            a1024, a1024_r, _ = palloc("a1024", BF16, [4, 128], parts=1)
            c1024, c1024_r, _ = palloc("c1024", BF16, [S], parts=1)
            dftb, dftb_r, _ = palloc("dftb", BF16, [2, 2, 8, 512], nres=2)
            dma("pool", c1024, c1024_d, writes=c1024_r)
            dft_ring = Ring(4, 2, lambda k, b: dma("sp", dftb[:, b].rearrange("p a c n -> p (a c n)"), dft_s[k],
                                                   reads=[dft_sr[k]], writes=[dftb_r[b]]))
            dft_ring.need(0)
            for gq in range(4):
                for tq in range(4):
                    bk, br = next_bank()
                    for dc in range(8):
                        mm(bk[:, :], wf[:, dc, gq * 128:(gq + 1) * 128], xT[:, dc, tq * 512:(tq + 1) * 512],
                           dc == 0, dc == 7, xT_r + wf_r, [br])
                    evac_copy(UT[:, gq, tq * 512:(tq + 1) * 512], bk[:, :], [br], [UT_r[gq * 4 + tq]], eng="act")
                    bfree(br)
            H = S // 2
            kb.op("dve", lambda e: e.tensor_tensor(out=UTe[:, :, 1:H], in0=UT[:, :, 1:H], in1=UT[:, :, S - 1:H:-1],
                                                   op=ALU.add), UT_r, UTe_r)
            kb.op("dve", lambda e: e.tensor_copy(out=UTe[:, :, 0:1], in_=UT[:, :, 0:1]), UT_r, UTe_r)
            kb.op("dve", lambda e: e.tensor_tensor(out=UTo[:, :, 1:H], in0=UT[:, :, 1:H], in1=UT[:, :, S - 1:H:-1],
                                                   op=ALU.subtract), UT_r, UTo_r)
            kb.op("dve", lambda e: e.memset(UTo[:, :, 0:1], 0.0), (), UTo_r)
            bk, br = next_bank()
            for gq in range(4):
                mm(bk[0:1, gq * 128:(gq + 1) * 128], UT[:, gq, H:H + 1], ccsc[:, 0:128], True, True,
                   UT_r + ccsc_r, [br], skip=True)
            evac_copy(a1024, bk[0:1, :].rearrange("p (g n) -> p g n", g=4), [br], a1024_r, eng="act")
            bfree(br)
            for c in range(8):
                for gp in range(2):
                    bk, br = next_bank()
                    for j in range(2):
                        gq = gp * 2 + j
                        mm(bk[:, j * 256:j * 256 + 128], UTe[:, gq, c * 128:(c + 1) * 128], ccsc[:, 0:128], True, True,
                           UTe_r + ccsc_r, [br], skip=True)
                        mm(bk[:, j * 256 + 128:(j + 1) * 256], UTo[:, gq, c * 128:(c + 1) * 128], ccsc[:, 128:256],
                           True, True, UTo_r + ccsc_r, [br], skip=True)
                    evac_copy(AB[:, c, gp * 2:gp * 2 + 2, :], bk[:, :].rearrange("p (j n) -> p j n", j=2), [br], AB_r)
                    bfree(br)
            for kt in range(4):
                fb = [next_bank() for _ in range(4)]
                dft_ring.need(kt)
                b = kt % 2
                for cs_ in range(2):
                    for c in range(8):
                        for gq in range(4):
                            mm(fb[gq][0][:, :], AB[:, c, gq, cs_ * 128:(cs_ + 1) * 128], dftb[:, b, cs_, c, :],
                               cs_ == 0 and c == 0, False, AB_r + [dftb_r[b]], [fb[gq][1]])
                for gq in range(4):
                    mm(fb[gq][0][:, :], a1024[0:1, gq, :], c1024[0:1, kt * 512:(kt + 1) * 512], False, True,
                       a1024_r + c1024_r, [fb[gq][1]])
                for gq in range(4):
                    evac_copy(reft[:, gq, kt * 512:(kt + 1) * 512], fb[gq][0][:, :], [fb[gq][1]], [reft_r[gq]])
                    bfree(fb[gq][1])
            mem.release()
            if debug is not None and debug[0] == "reft" and s == 0:
                finish_debug(reft, reft_r)

            mergedT, mergedT_r, mergedT_off = palloc("mergedT", BF16, [8, S], nres=8, high=True)
            wout, wout_r, _ = palloc("wout", BF16, [8, D], high=True)
            wup, wup_r, _ = palloc("wup", BF16, [2, 8, 512], nres=2, high=True)
            dma("sp", wout, wout_s, reads=wout_sr, writes=wout_r)
            wup_ring = Ring(32, 2, lambda k, b: dma("sp", wup[:, b], wup_s[k % 8], reads=[wup_sr[k % 8]],
                                                    writes=[wup_r[b]]))
            wup_ring.need(0)
            mem.mark()
            wg, wg_r, _ = palloc("wg", BF16, [4, 8, 128], nres=4)
            gts, gts_r, _ = palloc("gts", BF16, [2, 2, 512], nres=4)
            t12, t12_r, _ = palloc("t12", F32, [2, 2, 512], nres=4)
            wg_ring = Ring(8, 2, lambda k, b: dma(
                "pool", wg[:, 2 * b:2 * b + 2],
                wg_d.rearrange("(a c) p k n -> c p a k n", a=2)[k], writes=[wg_r[2 * b], wg_r[2 * b + 1]]))
            for dc in range(8):
                wg_ring.need(dc)
                wslots = [2 * (dc % 2), 2 * (dc % 2) + 1]
                for tq in range(4):
                    cs = slice(tq * 512, (tq + 1) * 512)
                    gbuf = (dc * 4 + tq) % 2
                    for af in range(2):
                        bk, br = next_bank()
                        for kc in range(8):
                            mm(bk[:, :], wg[:, wslots[af], kc, :], xT[:, kc, cs], kc == 0, kc == 7,
                               xT_r + [wg_r[wslots[af]]], [br])
                        kb.op("act", (lambda bk, gbuf, af: lambda e: e.activation(
                            out=gts[:, gbuf, af, :], in_=bk[:, :], func=AF.Sigmoid))(bk, gbuf, af),
                            [br], [gts_r[gbuf * 2 + af]])
                        bfree(br)
                    bka, bra = next_bank()
                    for hp in range(4):
                        mm(bka[:, :], wao[:, hp, dc * 128:(dc + 1) * 128], attnT[:, hp, cs], hp == 0, hp == 3,
                           attnT_r + wao_r, [bra])
                    kb.op("dve", (lambda bka, gbuf: lambda e: e.tensor_tensor(
                        out=t12[:, gbuf, 0, :], in0=bka[:, :], in1=gts[:, gbuf, 0, :], op=ALU.mult))(bka, gbuf),
                        [bra, gts_r[gbuf * 2]], [t12_r[gbuf * 2]])
                    bfree(bra)
                    bkf, brf = next_bank()
                    for gq in range(4):
                        mm(bkf[:, :], wfo[:, gq, dc * 128:(dc + 1) * 128], reft[:, gq, cs], gq == 0, gq == 3,
                           reft_r + wfo_r, [brf])
                    kb.op("dve", (lambda bkf, gbuf: lambda e: e.tensor_tensor(
                        out=t12[:, gbuf, 1, :], in0=bkf[:, :], in1=gts[:, gbuf, 1, :], op=ALU.mult))(bkf, gbuf),
                        [brf, gts_r[gbuf * 2 + 1]], [t12_r[gbuf * 2 + 1]])
                    bfree(brf)
                    kb.op("dve", (lambda gbuf, dc, cs: lambda e: e.tensor_tensor(
                        out=mergedT[:, dc, cs], in0=t12[:, gbuf, 0, :], in1=t12[:, gbuf, 1, :], op=ALU.add))(gbuf, dc, cs),
                        [t12_r[gbuf * 2], t12_r[gbuf * 2 + 1]], [mergedT_r[dc]])
            mem.release()
            mem.release()
            if debug is not None and debug[0] == "mergedT" and s == 0:
                finish_debug(mergedT, mergedT_r)

            mem.mark()
            xr, xr_r, _ = palloc("xr", F32, [3, D], nres=3)
            z1, z1_r, _ = palloc("z1", F32, [2, D], nres=2)
            ah1T, ah1T_r, _ = palloc("ah1T", F32, [2, 8, 512], nres=16)
            h1T, h1T_r, _ = palloc("h1T", BF16, [8, 512], nres=8)
            hidT, hidT_r, _ = palloc("hidT", BF16, [32, 512], nres=32)
            rr, rr_r, _ = palloc("rr", F32, [2, 512], nres=2)
            wdn, wdn_r, _ = palloc("wdn", BF16, [2, 32, 128], nres=2)
            zn2, zn2_r, zn2_off = palloc("zn2", F32, [3, D], nres=3)
            st, st_r, _ = palloc("st", F32, [4, 16], nres=4)
            xr_ring = Ring(16, 3, lambda k, b: dma("sp", xr[:, b, :], x_d[s, k * 128:(k + 1) * 128, :],
                                                   writes=[xr_r[b]]))
            wdn_ring = Ring(32, 2, lambda k, b: dma("sp", wdn[:, b], wdn_s[k % 8], reads=[wdn_sr[k % 8]],
                                                    writes=[wdn_r[b]]))
            xr_ring.need(0)
            wdn_ring.need(0)
            cnt = {"z": 0, "st": 0, "zn": 0}
            stores = []

            def ln_stats(src0, src1, sb, reads):
                for half, src in enumerate((src0, src1)):
                    kb.op("dve", (lambda src, half: lambda e: e.bn_stats(
                        out=st[:, sb, half * 6:half * 6 + 6], in_=src))(src, half), reads[half], [st_r[sb]])
                kb.op("dve", lambda e: e.bn_aggr(
                    out=st[:, sb, 12:14], in_=st[:, sb, 0:12].rearrange("p (a b) -> p a b", a=2)),
                    [st_r[sb]], [st_r[sb]])
                kb.op("dve", lambda e: e.tensor_scalar(
                    out=st[:, sb, 13:14], in0=st[:, sb, 13:14], scalar1=EPS, scalar2=None, op0=ALU.add),
                    [st_r[sb]], [st_r[sb]])
                kb.op("pool", lambda e: e.tensor_tensor(
                    out=st[:, sb, 14:15], in0=st[:, sb, 13:14], in1=mhalf[:, 0:1], op=ALU.pow),
                    [st_r[sb]] + mhalf_r, [st_r[sb]])
                kb.op("dve", lambda e: e.scalar_tensor_tensor(
                    out=st[:, sb, 15:16], in0=st[:, sb, 12:13], scalar=-1.0, in1=st[:, sb, 14:15],
                    op0=ALU.mult, op1=ALU.mult), [st_r[sb]], [st_r[sb]])

            def ln1_mix(tq, sub):
                tt = tq * 4 + sub
                tok = slice(tt * 128, (tt + 1) * 128)
                xr_ring.need(tt)
                xb = tt % 3
                zb = cnt["z"] % 2
                cnt["z"] += 1
                sb = cnt["st"] % 4
                cnt["st"] += 1
                for half in range(2):
                    bk, br = next_bank()
                    for dc in range(8):
                        mm(bk[:, :], mergedT[:, dc, tok], wout[:, dc, half * 512:(half + 1) * 512], dc == 0, dc == 7,
                           mergedT_r + wout_r, [br])
                    kb.op("dve", (lambda bk, half: lambda e: e.scalar_tensor_tensor(
                        out=z1[:, zb, half * 512:(half + 1) * 512], in0=xr[:, xb, half * 512:(half + 1) * 512],
                        scalar=ALPHA, in1=bk[:, :], op0=ALU.mult, op1=ALU.add))(bk, half),
                        [br, xr_r[xb]], [z1_r[zb]])
                    bfree(br)
                ln_stats(z1[:, zb, 0:512], z1[:, zb, 512:1024], sb, [[z1_r[zb]], [z1_r[zb]]])
                kb.op("act", lambda e: e.activation(
                    out=z1[:, zb, :], in_=z1[:, zb, :], func=AF.Identity, scale=st[:, sb, 14:15],
                    bias=st[:, sb, 15:16]), [st_r[sb], z1_r[zb]], [z1_r[zb]])
                return zb

            def ln1_tr(tq, sub, zb):
                ab = tq % 2
                for half in range(2):
                    bk, br = next_bank()
                    for j in range(4):
                        dc = half * 4 + j
                        tr(bk[:, j * 128:(j + 1) * 128], z1[:, zb, dc * 128:(dc + 1) * 128], [z1_r[zb]], [br])
                    for j in range(4):
                        dc = half * 4 + j
                        for (dst, dst_r, so, bo) in ((h1T[:, dc, sub * 128:(sub + 1) * 128], h1T_r[dc], G1, B1),
                                                     (ah1T[:, ab, dc, sub * 128:(sub + 1) * 128], ah1T_r[ab * 8 + dc], AG1, AB1)):
                            if half == 0:
                                kb.op("act", (lambda bk, j, dc, dst, so, bo: lambda e: e.activation(
                                    out=dst, in_=bk[:, j * 128:(j + 1) * 128],
                                    func=AF.Identity, scale=pvec[:, so + dc:so + dc + 1],
                                    bias=pvec[:, bo + dc:bo + dc + 1]))(bk, j, dc, dst, so, bo),
                                    [br] + pvec_r, [dst_r])
                            else:
                                kb.op("dve", (lambda bk, j, dc, dst, so, bo: lambda e: e.tensor_scalar(
                                    out=dst, in0=bk[:, j * 128:(j + 1) * 128],
                                    scalar1=pvec[:, so + dc:so + dc + 1], scalar2=pvec[:, bo + dc:bo + dc + 1],
                                    op0=ALU.mult, op1=ALU.add))(bk, j, dc, dst, so, bo),
                                    [br] + pvec_r, [dst_r])
                    bfree(br)

            def ln2(tq):
                for sub in range(4):
                    ln2_sub(tq, sub)

            def ln2_sub(tq, sub):
                ab = tq % 2
                if True:
                    tt = tq * 4 + sub
                    tok = slice(tt * 128, (tt + 1) * 128)
                    zb = cnt["zn"] % 3
                    cnt["zn"] += 1
                    sb = cnt["st"] % 4
                    cnt["st"] += 1
                    hb = []
                    for half in range(2):
                        bk, br = next_bank()
                        hb.append((bk, br))
                        for j in range(4):
                            dc = half * 4 + j
                            tr(bk[:, j * 128:(j + 1) * 128], ah1T[:, ab, dc, sub * 128:(sub + 1) * 128],
                               [ah1T_r[ab * 8 + dc]], [br])
                    ln_stats(hb[0][0][:, :], hb[1][0][:, :], sb, [[hb[0][1]], [hb[1][1]]])
                    bk, br = hb[0]
                    kb.op("dve", (lambda bk: lambda e: e.scalar_tensor_tensor(
                        out=zn2[:, zb, 0:512], in0=bk[:, :], scalar=st[:, sb, 12:13], in1=g2b[:, 0:512],
                        op0=ALU.subtract, op1=ALU.mult))(bk), [br, st_r[sb]] + g2b_r, [zn2_r[zb]])
                    bfree(br)
                    kb.op("dve", lambda e: e.scalar_tensor_tensor(
                        out=zn2[:, zb, 0:512], in0=zn2[:, zb, 0:512], scalar=st[:, sb, 14:15], in1=b2b[:, 0:512],
                        op0=ALU.mult, op1=ALU.add), [zn2_r[zb], st_r[sb]] + b2b_r, [zn2_r[zb]])
                    bk, br = hb[1]
                    kb.op("act", (lambda bk: lambda e: e.activation(
                        out=zn2[:, zb, 512:1024], in_=bk[:, :], func=AF.Identity,
                        scale=st[:, sb, 14:15], bias=st[:, sb, 15:16]))(bk), [br, st_r[sb]], [zn2_r[zb]])
                    bfree(br)
                    kb.op("pool", lambda e: e.tensor_tensor(
                        out=zn2[:, zb, 512:1024], in0=zn2[:, zb, 512:1024], in1=g2b[:, 512:1024], op=ALU.mult),
                        [zn2_r[zb]] + g2b_r, [zn2_r[zb]])
                    kb.op("pool", lambda e: e.tensor_tensor(
                        out=zn2[:, zb, 512:1024], in0=zn2[:, zb, 512:1024], in1=b2b[:, 512:1024], op=ALU.add),
                        [zn2_r[zb]] + b2b_r, [zn2_r[zb]])
                    stores.append((lambda dst, srcap, rs: lambda: dma("sp", dst, srcap, reads=rs, is_out=True))(
                        out_d[s, tok, :], zn2[:, zb, :], [zn2_r[zb]]))

            def up_fg(tq, fg):
                k = tq * 8 + fg
                wup_ring.need(k)
                ub = k % 2
                for j in range(4):
                    fc = fg * 4 + j
                    bk, br = next_bank()
                    for dc in range(8):
                        mm(bk[:, :], wup[:, ub, dc, j * 128:(j + 1) * 128], h1T[:, dc, :], dc == 0, dc == 7,
                           h1T_r + [wup_r[ub]], [br])
                    rb = fc % 2
                    kb.op("dve", (lambda bk, rb, fc: lambda e: e.tensor_scalar(
                        out=rr[:, rb, :], in0=bk[:, :], scalar1=pvec[:, BUP + fc:BUP + fc + 1], scalar2=0.0,
                        op0=ALU.add, op1=ALU.max))(bk, rb, fc), [br] + pvec_r, [rr_r[rb]])
                    bfree(br)
                    kb.op("act", (lambda rb, fc: lambda e: e.activation(
                        out=hidT[:, fc, :], in_=rr[:, rb, :], func=AF.Square))(rb, fc), [rr_r[rb]], [hidT_r[fc]])

            def down_dc(tq, dc):
                ab = tq % 2
                k = tq * 8 + dc
                wdn_ring.need(k)
                db = k % 2
                bk, br = next_bank()
                for fc in range(32):
                    mm(bk[:, :], wdn[:, db, fc, :], hidT[:, fc, :], fc == 0, fc == 31, hidT_r + [wdn_r[db]], [br])
                kb.op("dve", (lambda bk, dc, ab: lambda e: e.scalar_tensor_tensor(
                    out=ah1T[:, ab, dc, :], in0=bk[:, :], scalar=pvec[:, BDN + dc:BDN + dc + 1], in1=ah1T[:, ab, dc, :],
                    op0=ALU.add, op1=ALU.add))(bk, dc, ab), [br, ah1T_r[ab * 8 + dc]] + pvec_r, [ah1T_r[ab * 8 + dc]])
                bfree(br)

            pend = None
            for sub in range(4):
                zb = ln1_mix(0, sub)
                if pend is not None:
                    ln1_tr(0, pend[0], pend[1])
                pend = (sub, zb)
            ln1_tr(0, pend[0], pend[1])
            if debug is not None and debug[0] == "h1T":
                finish_debug(h1T, h1T_r)
            for tq in range(4):
                for fg in range(8):
                    up_fg(tq, fg)
                    if tq > 0 and 2 <= fg < 6:
                        stores.pop(0)()
                    if tq > 0 and fg < 4:
                        ln2_sub(tq - 1, fg)
                zbs = {}
                nxt_x = None
                if tq == 3 and s + 1 < nseq:
                    nxs_r = mem.alloc_over("xs_n", mergedT_off, 3 * D * 4, 3)
                    nxs = mem.view(mergedT_off, F32, [3, D])
                    nxT_r = mem.alloc_over("xT_n", xT_off, 8 * S * 2, 16)
                    nxT = mem.view(xT_off, BF16, [8, S])
                    nring = Ring(16, 3, (lambda nxs, nxs_r: lambda k, b: dma(
                        "sp", nxs[:, b, :], x_d[s + 1, k * 128:(k + 1) * 128, :], writes=[nxs_r[b]]))(nxs, nxs_r))
                    nring.need(0)
                    nxt_x = (nxT, nxT_r, nxs, nxs_r, nring)
                    pre["xT"] = nxT_r
                    nwv_r = mem.alloc_over("wv_n", wv_off, 2 * 8 * 512 * 2, 2)
                    nwv = mem.view(wv_off, BF16, [2, 8, 512])
                    for g_ in range(2):
                        dma("pool", nwv[:, g_], wv_d[g_], writes=[nwv_r[g_]])
                    pre["wv"] = nwv_r
                for dc in range(8):
                    down_dc(tq, dc)
                    if nxt_x is not None:
                        for tt in (2 * dc, 2 * dc + 1):
                            phase0_tile(s + 1, tt, nxt_x[0], nxt_x[1], nxt_x[2], nxt_x[3], nxt_x[4])
                    if tq < 3:
                        if 1 <= dc <= 4:
                            ln1_tr(tq + 1, dc - 1, zbs[dc - 1])
                        if dc < 4:
                            zbs[dc] = ln1_mix(tq + 1, dc)
            for sub in range(4):
                ln2_sub(3, sub)
                if len(stores) > 1:
                    stores.pop(0)()
            if s + 1 < nseq:
                late_stores.extend([list(stores), zn2_off, zn2_off + 3 * D * 4])
                stores = []
            while stores:
                stores.pop(0)()
            mem.release()
            mem.release()
            for reg in mem.regions:
                if reg[0] >= mem.htop:
                    reg[3] = False
            mem.htop = POOL_BYTES
    except _Stop:
        pass

    kb.emit()
    return nc


def host_inputs(x, rel_bias, w_in, w_fourier_out, w_attn_out, w_out, ln1_g, ln1_b, w_up, b_up, w_down, b_down,
                ln2_g, ln2_b):
    f32 = np.float32
    w_in = np.asarray(w_in, f32)[0]
    Q, K_, V_, F_, GA, GF = 0, 1536, 3072, 4608, 5120, 6144

    def tile_w(w):
        return np.ascontiguousarray(w.reshape(8, 128, -1).transpose(1, 0, 2))

    wqk = np.stack([np.stack([tile_w(w_in[:, Q + 128 * p:Q + 128 * (p + 1)]),
                              tile_w(w_in[:, K_ + 128 * p:K_ + 128 * (p + 1)])]) for p in range(12)])
    wv = np.stack([tile_w(w_in[:, V_ + 512 * g:V_ + 512 * (g + 1)]) for g in range(3)])
    wf = tile_w(w_in[:, F_:F_ + 512])
    wg = np.stack([tile_w(w_in[:, GA + 128 * i:GA + 128 * (i + 1)]) for i in range(16)])
    wao = np.ascontiguousarray(np.asarray(w_attn_out, f32)[0].reshape(4, 128, D).transpose(1, 0, 2))
    wfo = np.ascontiguousarray(np.asarray(w_fourier_out, f32)[0].reshape(4, 128, D).transpose(1, 0, 2))
    wout = tile_w(np.asarray(w_out, f32)[0])
    wu = np.asarray(w_up, f32)[0]
    wup = np.stack([tile_w(wu[:, 512 * i:512 * (i + 1)]) for i in range(8)])
    wd = np.asarray(w_down, f32)[0]
    wdn = np.stack([np.ascontiguousarray(wd[:, 128 * dc:128 * (dc + 1)].reshape(32, 128, 128).transpose(1, 0, 2))
                    for dc in range(8)])
    rb = np.asarray(rel_bias, f32)
    i = np.arange(128)[:, None]
    j = np.arange(256)[None, :]
    rel = i - j + 64
    valid = np.abs(rel) <= 64
    tab = np.empty((128, 24, 256), f32)
    for g, (win, d) in enumerate(GROUPS):
        bk = t5_buckets(rel * d)
        for hh in range(8):
            h = 8 * g + hh
            tab[:, h, :] = np.where(valid, rb[bk, h], f32(NEG))
    n = np.arange(S, dtype=np.int64)
    ang = 2.0 * np.pi * ((n[:, None] * n[None, :]) % S).astype(np.float64) / S
    cs = np.stack([np.cos(ang), np.sin(ang)]) / np.sqrt(S)
    c1024 = np.ascontiguousarray(cs[0, S // 2:S // 2 + 1, :]).astype(f32)
    dft = cs[:, :S // 2].reshape(2, 8, 128, 4, 512).transpose(3, 2, 0, 1, 4)
    dft = np.ascontiguousarray(dft).astype(f32)
    m = np.arange(128, dtype=np.int64)
    a2 = 2.0 * np.pi * ((m[:, None] * m[None, :]) % 128).astype(np.float64) / 128
    ccsc = (np.concatenate([np.cos(a2), -np.sin(a2)], axis=1) / np.sqrt(128.0)).astype(f32)
    identf = np.eye(128, dtype=f32)
    identb = np.eye(128).astype(ml_dtypes.bfloat16)

    def pp(v):
        return np.asarray(v, f32).reshape(-1, 128).T

    g1, b1 = np.asarray(ln1_g, f32)[0], np.asarray(ln1_b, f32)[0]
    pvec_parts = [pp(g1), pp(b1), None, None, pp(np.asarray(b_down, f32)[0]), pp(np.asarray(b_up, f32)[0])]
    shared = dict(wqk=wqk, wv=wv, wf=wf, wg=wg, wao=wao, wfo=wfo, wout=wout, wup=wup, wdn=wdn, tab=tab, dft=dft,
                  ccsc=ccsc, c1024=c1024, identf=identf,
                  dft_s=np.zeros((4, 128, 2 * 8 * 512), ml_dtypes.bfloat16), identb=identb,
                  g2b=np.ascontiguousarray(np.broadcast_to(np.asarray(ln2_g, f32)[0], (128, D))),
                  b2b=np.ascontiguousarray(np.broadcast_to(np.asarray(ln2_b, f32)[0], (128, D))))
    return shared, pvec_parts


_CACHE = {}


def kernel(x, rel_bias, w_in, w_fourier_out, w_attn_out, w_out, ln1_g, ln1_b, w_up, b_up, w_down, b_down,
           ln2_g, ln2_b, _debug=None, _ncores=NCORES):
    x = np.asarray(x, np.float32)
    shared, pv = host_inputs(x, rel_bias, w_in, w_fourier_out, w_attn_out, w_out, ln1_g, ln1_b, w_up, b_up,
                             w_down, b_down, ln2_g, ln2_b)
    pvec = np.zeros((128, 72), np.float32)
    pvec[:, 0:8] = pv[0]
    pvec[:, 8:16] = pv[1]
    pvec[:, 32:40] = pv[4]
    pvec[:, 40:72] = pv[5]
    shared["pvec"] = pvec
    nc = build(debug=_debug)
    in_maps = []
    for c in range(_ncores):
        m = dict(shared)
        m["x"] = np.ascontiguousarray(x[2 * c:2 * c + 2])
        in_maps.append(m)
    res = run_bass_kernel_spmd(nc, in_maps, core_ids=list(range(_ncores)))
    if _debug is not None:
        return res.results
    return np.concatenate([r["out"] for r in res.results], axis=0)
```
